# Optimizing a Trainium2 kernel written in Bass

```python
import jax, jax.numpy as jnp
from jax import lax
import numpy as np

D_MODEL = 4096
BATCH = 4
SEQ = 4096
DEPTH = 4

N_MIXERS = 4
HEAD_DIM = 128
N_HEADS = D_MODEL // HEAD_DIM
ROPE_THETA = 500000.0
ROPE_DIM = HEAD_DIM // 4
NORM_EPS = 1e-6
BIG = 1e30
NSA_KV_GROUPS = 4
NSA_HPG = N_HEADS // NSA_KV_GROUPS
CMP_BLOCK = 32
CMP_STRIDE = 16
SEL_BLOCK = 64
SEL_TOPK = 16
WINDOW = 512
NSA_QBLOCK = 32
NSA_IN = D_MODEL + 6 * NSA_KV_GROUPS * HEAD_DIM + 3 * N_HEADS
CONV_WIDTH = 31
CONV_IN = 2 * D_MODEL
SB_QBLOCK = 128
SB_IN = 3 * D_MODEL
GM_CHUNK = 128
GM_GROUP_CH = 128
GM_GROUPS = D_MODEL // GM_GROUP_CH
GM_IN = 2 * D_MODEL
MEM_LEN = 256
MEM_HEADS = 4
MEM_WIDTH = D_MODEL // 4
MEM_HEAD_DIM = MEM_WIDTH // MEM_HEADS
OUT_IN = D_MODEL + MEM_WIDTH
D_FF = (43 * D_MODEL) // 32

kernel_name = "hybrid_nsa_conv_stickbreak_gmlp_decoder"


def rms_norm(x, g):
    xf = x.astype(jnp.float32)
    y = xf * lax.rsqrt(jnp.mean(xf * xf, axis=-1, keepdims=True) + NORM_EPS)
    return (y * g.astype(jnp.float32)).astype(x.dtype)


def layer_norm(x, g, b):
    xf = x.astype(jnp.float32)
    xc = xf - jnp.mean(xf, axis=-1, keepdims=True)
    y = xc * lax.rsqrt(jnp.mean(xc * xc, axis=-1, keepdims=True) + NORM_EPS)
    return (y * g.astype(jnp.float32) + b.astype(jnp.float32)).astype(x.dtype)


def masked_softmax(s, mask):
    p = jax.nn.softmax(jnp.where(mask, s, -BIG), axis=-1)
    return p * mask


def rope_tables(positions):
    inv = 1.0 / (ROPE_THETA ** (jnp.arange(0, ROPE_DIM, 2, dtype=jnp.float32) / ROPE_DIM))
    ang = positions.astype(jnp.float32)[..., None] * inv
    return jnp.cos(ang), jnp.sin(ang)


def apply_partial_rope(x, cos, sin):
    half = ROPE_DIM // 2
    x1, x2, rest = x[..., :half], x[..., half:ROPE_DIM], x[..., ROPE_DIM:]
    c = cos[:, :, None, :].astype(x.dtype)
    s = sin[:, :, None, :].astype(x.dtype)
    return jnp.concatenate([x1 * c - x2 * s, x2 * c + x1 * s, rest], axis=-1)


def swiglu(h, w_gu, w_down):
    gu = h @ w_gu
    g, u = gu[..., :D_FF], gu[..., D_FF:]
    return (jax.nn.silu(g) * u) @ w_down


def nsa_compress(x, pos_emb, w1, w2):
    B, T, G, dh = x.shape
    r = CMP_BLOCK // CMP_STRIDE
    n_pieces = T // CMP_STRIDE
    n_cmp = n_pieces - r + 1
    pieces = x.reshape(B, n_pieces, CMP_STRIDE, G, dh)
    blocks = jnp.concatenate([pieces[:, k:k + n_cmp] for k in range(r)], axis=2)
    blocks = blocks + pos_emb[:, None, :]
    flat = jnp.moveaxis(blocks, 2, 3).reshape(B, n_cmp, G, CMP_BLOCK * dh)
    return jax.nn.silu(flat @ w1) @ w2


def nsa_mixer(p, cos, sin, q_norm, k_norm, cmp_pos, cmp_w1, cmp_w2):
    B, T, _ = p.shape
    G, J, dh = NSA_KV_GROUPS, NSA_HPG, HEAD_DIM
    kv = G * dh
    q = p[..., :D_MODEL].reshape(B, T, N_HEADS, dh)
    kc, vc, ks, vs, kw, vw = [p[..., D_MODEL + i * kv:D_MODEL + (i + 1) * kv].reshape(B, T, G, dh) for i in range(6)]
    gates = jax.nn.sigmoid(p[..., D_MODEL + 6 * kv:]).reshape(B, T, N_HEADS, 3)
    q = apply_partial_rope(rms_norm(q, q_norm), cos, sin)
    kc = apply_partial_rope(rms_norm(kc, k_norm[0]), cos, sin)
    ks = apply_partial_rope(rms_norm(ks, k_norm[1]), cos, sin)
    kw = apply_partial_rope(rms_norm(kw, k_norm[2]), cos, sin)
    k_cmp = nsa_compress(kc, cmp_pos[0], cmp_w1[0], cmp_w2[0])
    v_cmp = nsa_compress(vc, cmp_pos[1], cmp_w1[1], cmp_w2[1])
    n_cmp = k_cmp.shape[1]
    cmp_end = jnp.arange(n_cmp) * CMP_STRIDE + (CMP_BLOCK - 1)
    n_sel = T // SEL_BLOCK
    n_top = min(SEL_TOPK, n_sel)
    rs = SEL_BLOCK // CMP_STRIDE
    r = CMP_BLOCK // CMP_STRIDE
    k_blk = ks.reshape(B, n_sel, SEL_BLOCK, G, dh).transpose(0, 3, 1, 2, 4)
    v_blk = vs.reshape(B, n_sel, SEL_BLOCK, G, dh).transpose(0, 3, 1, 2, 4)
    blk_start = jnp.arange(n_sel) * SEL_BLOCK
    jj = jnp.arange(n_sel)
    b_idx = jnp.arange(B)[:, None, None]
    g_idx = jnp.arange(G)[None, :, None]
    kw_pad = jnp.pad(kw, ((0, 0), (WINDOW, 0), (0, 0), (0, 0)))
    vw_pad = jnp.pad(vw, ((0, 0), (WINDOW, 0), (0, 0), (0, 0)))
    scale = dh ** -0.5
    QB = NSA_QBLOCK

    def block(bi):
        q0 = bi * QB
        t = q0 + jnp.arange(QB)
        qb = lax.dynamic_slice_in_dim(q, q0, QB, axis=1).reshape(B, QB, G, J, dh)
        gb = lax.dynamic_slice_in_dim(gates, q0, QB, axis=1).reshape(B, QB, G, J, 3)
        s_c = jnp.einsum('bqgjd,bngd->bqgjn', qb, k_cmp).astype(jnp.float32) * scale
        p_c = masked_softmax(s_c, (cmp_end[None, :] <= t[:, None])[None, :, None, None, :])
        o_c = jnp.einsum('bqgjn,bngd->bqgjd', p_c.astype(v_cmp.dtype), v_cmp)
        imp = jnp.pad(p_c.sum(axis=3), ((0, 0), (0, 0), (0, 0), (0, n_sel * rs - n_cmp)))
        grp = imp.reshape(B, QB, G, n_sel, rs)
        imp_sel = grp.sum(axis=-1)
        for k in range(1, r):
            imp_sel = imp_sel + jnp.pad(grp[..., rs - k], ((0, 0), (0, 0), (0, 0), (1, 0)))[..., :-1]
        cur = t // SEL_BLOCK
        forced = (jj[None, :] == 0) | (jj[None, :] == cur[:, None]) | (jj[None, :] == cur[:, None] - 1)
        valid = blk_start[None, :] <= t[:, None]
        score = jnp.where(forced[None, :, None, :], BIG, imp_sel)
        score = jnp.where(valid[None, :, None, :], score, -BIG)
        _, top = lax.top_k(score, n_top)
        top_g = top.transpose(0, 2, 1, 3).reshape(B, G, QB * n_top)
        k_sel = k_blk[b_idx, g_idx, top_g].reshape(B, G, QB, n_top * SEL_BLOCK, dh)
        v_sel = v_blk[b_idx, g_idx, top_g].reshape(B, G, QB, n_top * SEL_BLOCK, dh)
        s_s = jnp.einsum('bqgjd,bgqmd->bqgjm', qb, k_sel).astype(jnp.float32) * scale
        tok = (top[..., None] * SEL_BLOCK + jnp.arange(SEL_BLOCK)).reshape(B, QB, G, n_top * SEL_BLOCK)
        p_s = masked_softmax(s_s, (tok <= t[None, :, None, None])[:, :, :, None, :])
        o_s = jnp.einsum('bqgjm,bgqmd->bqgjd', p_s.astype(v_sel.dtype), v_sel)
        kwin = lax.dynamic_slice_in_dim(kw_pad, q0, WINDOW + QB, axis=1)
        vwin = lax.dynamic_slice_in_dim(vw_pad, q0, WINDOW + QB, axis=1)
        kpos = q0 - WINDOW + jnp.arange(WINDOW + QB)
        diff = t[:, None] - kpos[None, :]
        m_w = (kpos[None, :] >= 0) & (diff >= 0) & (diff < WINDOW)
        s_w = jnp.einsum('bqgjd,bkgd->bqgjk', qb, kwin).astype(jnp.float32) * scale
        p_w = masked_softmax(s_w, m_w[None, :, None, None, :])
        o_w = jnp.einsum('bqgjk,bkgd->bqgjd', p_w.astype(vwin.dtype), vwin)
        o = gb[..., 0:1] * o_c + gb[..., 1:2] * o_s + gb[..., 2:3] * o_w
        return o.reshape(B, QB, N_HEADS * dh)

    out = lax.map(block, jnp.arange(T // QB))
    return out.transpose(1, 0, 2, 3).reshape(B, T, N_HEADS * dh)


def conformer_conv_mixer(p, b_in, dw_w, dw_b, ln_g, ln_b):
    p = p + b_in
    u = p[..., :D_MODEL] * jax.nn.sigmoid(p[..., D_MODEL:])
    y = lax.conv_general_dilated(u, dw_w[:, None, :], window_strides=(1,), padding=[(CONV_WIDTH - 1, 0)],
                                 dimension_numbers=('NWC', 'WIO', 'NWC'), feature_group_count=D_MODEL)
    y = layer_norm(y + dw_b, ln_g, ln_b)
    return jax.nn.silu(y)


def stick_breaking_mixer(p):
    B, T, _ = p.shape
    q = p[..., :D_MODEL].reshape(B, T, N_HEADS, HEAD_DIM)
    k = p[..., D_MODEL:2 * D_MODEL].reshape(B, T, N_HEADS, HEAD_DIM)
    v = p[..., 2 * D_MODEL:].reshape(B, T, N_HEADS, HEAD_DIM)
    scale = HEAD_DIM ** -0.5
    s_pos = jnp.arange(T)

    def block(bi):
        q0 = bi * SB_QBLOCK
        t = q0 + jnp.arange(SB_QBLOCK)
        qb = lax.dynamic_slice_in_dim(q, q0, SB_QBLOCK, axis=1)
        z = jnp.einsum('bqhd,bkhd->bhqk', qb, k).astype(jnp.float32) * scale
        strict = (s_pos[None, :] < t[:, None])[None, None]
        log_beta = jax.nn.log_sigmoid(z)
        log_keep = jnp.where(strict, jax.nn.log_sigmoid(-z), 0.0)
        after = lax.cumsum(log_keep, axis=3, reverse=True) - log_keep
        a = jnp.where(strict, jnp.exp(log_beta + after), 0.0)
        o = jnp.einsum('bhqk,bkhd->bqhd', a.astype(v.dtype), v)
        return o.reshape(B, SB_QBLOCK, D_MODEL)

    out = lax.map(block, jnp.arange(T // SB_QBLOCK))
    return out.transpose(1, 0, 2, 3).reshape(B, T, D_MODEL)


def gmlp_mixer(p, ln_g, ln_b, w_s, b_s):
    B, T, _ = p.shape
    z = jax.nn.gelu(p, approximate=False)
    u, v = z[..., :D_MODEL], z[..., D_MODEL:]
    v = layer_norm(v, ln_g, ln_b)
    vc = v.reshape(B, T // GM_CHUNK, GM_CHUNK, GM_GROUPS, GM_GROUP_CH)
    causal = jnp.tril(jnp.ones((GM_CHUNK, GM_CHUNK), dtype=bool))
    ws = jnp.where(causal, w_s, 0)
    mixed = jnp.einsum('gts,bnsgc->bntgc', ws, vc) + b_s.T[None, None, :, :, None]
    return u * mixed.reshape(B, T, D_MODEL)


def memory_attention(mq, mk, mv, q_norm):
    B, T = mq.shape[0], mq.shape[1]
    mq = rms_norm(mq, q_norm)
    s = jnp.einsum('bthd,bmhd->bhtm', mq, mk).astype(jnp.float32) * (MEM_HEAD_DIM ** -0.5)
    pr = jax.nn.softmax(s, axis=-1)
    o = jnp.einsum('bhtm,bmhd->bthd', pr.astype(mv.dtype), mv)
    return o.reshape(B, T, MEM_WIDTH)


def setup_inputs(seed: int = 0) -> dict:
    key = jax.random.key(seed)
    ks = iter(jax.random.split(key, 48))
    f32 = jnp.float32

    def nrm(shape, fan_in):
        return jax.random.normal(next(ks), shape, f32) * (fan_in ** -0.5)

    def gain(shape):
        return 1.0 + 0.02 * jax.random.normal(next(ks), shape, f32)

    def small(shape, s=0.02):
        return s * jax.random.normal(next(ks), shape, f32)

    nA = len(range(0, DEPTH, N_MIXERS))
    nB = len(range(1, DEPTH, N_MIXERS))
    nC = len(range(2, DEPTH, N_MIXERS))
    nD = len(range(3, DEPTH, N_MIXERS))
    kv = NSA_KV_GROUPS * HEAD_DIM
    return {
        "x": jax.random.normal(next(ks), (BATCH, SEQ, D_MODEL), f32),
        "mem": jax.random.normal(next(ks), (BATCH, MEM_LEN, D_MODEL), f32),
        "positions": jax.random.randint(next(ks), (BATCH, 1), 0, 1024, dtype=jnp.int32) + jnp.arange(SEQ, dtype=jnp.int32)[None, :],
        "ffn1_norm": gain((DEPTH, D_MODEL)),
        "ffn1_w_gu": nrm((DEPTH, D_MODEL, 2 * D_FF), D_MODEL),
        "ffn1_w_down": nrm((DEPTH, D_FF, D_MODEL), D_FF),
        "mix_norm": gain((DEPTH, D_MODEL)),
        "ffn2_norm": gain((DEPTH, D_MODEL)),
        "ffn2_w_gu": nrm((DEPTH, D_MODEL, 2 * D_FF), D_MODEL),
        "ffn2_w_down": nrm((DEPTH, D_FF, D_MODEL), D_FF),
        "mem_norm": gain((D_MODEL,)),
        "mem_w_kv": nrm((D_MODEL, 2 * MEM_WIDTH), D_MODEL),
        "mem_k_norm": gain((MEM_HEAD_DIM,)),
        "mem_q_norm": gain((DEPTH, MEM_HEAD_DIM)),
        "nsa_w_in": nrm((nA, D_MODEL, NSA_IN + MEM_WIDTH), D_MODEL),
        "nsa_q_norm": gain((nA, HEAD_DIM)),
        "nsa_k_norm": gain((nA, 3, HEAD_DIM)),
        "nsa_cmp_pos": small((nA, 2, CMP_BLOCK, HEAD_DIM), 0.2),
        "nsa_cmp_w1": nrm((nA, 2, CMP_BLOCK * HEAD_DIM, HEAD_DIM), CMP_BLOCK * HEAD_DIM),
        "nsa_cmp_w2": nrm((nA, 2, HEAD_DIM, HEAD_DIM), HEAD_DIM),
        "nsa_w_out": nrm((nA, OUT_IN, D_MODEL), OUT_IN),
        "conv_w_in": nrm((nB, D_MODEL, CONV_IN + MEM_WIDTH), D_MODEL),
        "conv_b_in": small((nB, CONV_IN)),
        "conv_dw_w": nrm((nB, CONV_WIDTH, D_MODEL), CONV_WIDTH),
        "conv_dw_b": small((nB, D_MODEL)),
        "conv_ln_g": gain((nB, D_MODEL)),
        "conv_ln_b": small((nB, D_MODEL)),
        "conv_w_out": nrm((nB, OUT_IN, D_MODEL), OUT_IN),
        "sb_w_in": nrm((nC, D_MODEL, SB_IN + MEM_WIDTH), D_MODEL),
        "sb_w_out": nrm((nC, OUT_IN, D_MODEL), OUT_IN),
        "gm_w_in": nrm((nD, D_MODEL, GM_IN + MEM_WIDTH), D_MODEL),
        "gm_ln_g": gain((nD, D_MODEL)),
        "gm_ln_b": small((nD, D_MODEL)),
        "gm_ws": nrm((nD, GM_GROUPS, GM_CHUNK, GM_CHUNK), GM_CHUNK),
        "gm_bs": gain((nD, GM_GROUPS, GM_CHUNK)),
        "gm_w_out": nrm((nD, OUT_IN, D_MODEL), OUT_IN),
    }


def reference(x, mem, positions, ffn1_norm, ffn1_w_gu, ffn1_w_down, mix_norm, ffn2_norm, ffn2_w_gu, ffn2_w_down,
              mem_norm, mem_w_kv, mem_k_norm, mem_q_norm,
              nsa_w_in, nsa_q_norm, nsa_k_norm, nsa_cmp_pos, nsa_cmp_w1, nsa_cmp_w2, nsa_w_out,
              conv_w_in, conv_b_in, conv_dw_w, conv_dw_b, conv_ln_g, conv_ln_b, conv_w_out,
              sb_w_in, sb_w_out, gm_w_in, gm_ln_g, gm_ln_b, gm_ws, gm_bs, gm_w_out):
    B, T, _ = x.shape
    cos, sin = rope_tables(positions)
    mkv = rms_norm(mem, mem_norm) @ mem_w_kv
    mk = rms_norm(mkv[..., :MEM_WIDTH].reshape(B, MEM_LEN, MEM_HEADS, MEM_HEAD_DIM), mem_k_norm)
    mv = mkv[..., MEM_WIDTH:].reshape(B, MEM_LEN, MEM_HEADS, MEM_HEAD_DIM)
    for i in range(DEPTH):
        kind, j = i % N_MIXERS, i // N_MIXERS
        x = x + 0.5 * swiglu(rms_norm(x, ffn1_norm[i]), ffn1_w_gu[i], ffn1_w_down[i])
        h = rms_norm(x, mix_norm[i])
        if kind == 0:
            proj = h @ nsa_w_in[j]
            mixed = nsa_mixer(proj[..., :-MEM_WIDTH], cos, sin, nsa_q_norm[j], nsa_k_norm[j],
                              nsa_cmp_pos[j], nsa_cmp_w1[j], nsa_cmp_w2[j])
            w_out = nsa_w_out[j]
        elif kind == 1:
            proj = h @ conv_w_in[j]
            mixed = conformer_conv_mixer(proj[..., :-MEM_WIDTH], conv_b_in[j], conv_dw_w[j], conv_dw_b[j],
                                         conv_ln_g[j], conv_ln_b[j])
            w_out = conv_w_out[j]
        elif kind == 2:
            proj = h @ sb_w_in[j]
            mixed = stick_breaking_mixer(proj[..., :-MEM_WIDTH])
            w_out = sb_w_out[j]
        else:
            proj = h @ gm_w_in[j]
            mixed = gmlp_mixer(proj[..., :-MEM_WIDTH], gm_ln_g[j], gm_ln_b[j], gm_ws[j], gm_bs[j])
            w_out = gm_w_out[j]
        mq = proj[..., -MEM_WIDTH:].reshape(B, T, MEM_HEADS, MEM_HEAD_DIM)
        mem_out = memory_attention(mq, mk, mv, mem_q_norm[i])
        x = x + jnp.concatenate([mixed, mem_out], axis=-1) @ w_out
        x = x + 0.5 * swiglu(rms_norm(x, ffn2_norm[i]), ffn2_w_gu[i], ffn2_w_down[i])
    return x
```

```python
import numpy as np
from contextlib import ExitStack
import concourse.bass as bass
import concourse.mybir as mybir
from concourse.bass_utils import run_bass_kernel_spmd

F32 = mybir.dt.float32
BF16 = mybir.dt.bfloat16
I32 = mybir.dt.int32
AF = mybir.ActivationFunctionType
ALU = mybir.AluOpType
AX = mybir.AxisListType
BIG = 1e30
EPS = 1e-6
NCORES = 8
import os as _os
NOCC = bool(_os.environ.get("KNOCC"))


def make_cfg(D=4096, T=4096, DFF=5504, MH=4, B=4):
    c = dict(D=D, T=T, DFF=DFF, MH=MH, B=B)
    c["NH"] = D // 128
    c["G"] = 4
    c["HPG"] = c["NH"] // 4
    c["MEMW"] = D // 4
    c["MHD"] = c["MEMW"] // MH
    c["MDC"] = c["MHD"] // 128
    c["MEML"] = 256
    c["NB"] = T // 256
    c["TL"] = c["NB"] * 128
    c["KC"] = D // 128
    c["FC"] = DFF // 128
    c["OUTIN"] = D + c["MEMW"]
    c["NSA_IN"] = D + 6 * 512 + 3 * c["NH"]
    c["NCMP"] = T // 16 - 1
    c["NSEL"] = T // 64
    c["NIN"] = [c["NSA_IN"] + c["MEMW"], 2 * D + c["MEMW"], 3 * D + c["MEMW"], 2 * D + c["MEMW"]]
    return c


class Res:
    __slots__ = ("name", "w", "r")

    def __init__(self, name=""):
        self.name = name
        self.w = None
        self.r = {}


class Prog:
    NDMA = 16
    NPD = 8

    def __init__(self, nc, es):
        self.nc = nc
        self.engs = ["pe", "act", "dve", "pool", "sp"]
        self.ops = {k: [] for k in self.engs}
        self.seq = {k: 0 for k in self.engs}
        self.waited = {k: {} for k in self.engs}
        self.sems = {}
        for k in self.engs:
            self.sems[k] = es.enter_context(nc.semaphore("s_" + k))
        for i in range(self.NDMA):
            self.sems[("dma", i)] = es.enter_context(nc.semaphore("s_dma%d" % i))
        for i in range(self.NPD):
            self.sems[("pdma", i)] = es.enter_context(nc.semaphore("s_pdma%d" % i))
        self.sems["cc"] = es.enter_context(nc.semaphore("s_cc"))
        self.tot = {}
        self.rr = {"dma": 0, "pdma": 0}
        self.out_tokens = []
        self.phase_dma = {}

    def _deps(self, eng, reads, writes):
        deps = {}

        def add(tok):
            if tok is None:
                return
            k, v = tok
            if k == "pe" and eng == "pe":
                return
            if deps.get(k, 0) < v:
                deps[k] = v
        for R in reads:
            add(R.w)
        for R in writes:
            add(R.w)
            for k, v in R.r.items():
                add((k, v))
        waits = []
        wd = self.waited[eng]
        for k, v in deps.items():
            if wd.get(k, 0) < v:
                wd[k] = v
                waits.append((k, v))
        return waits

    def _commit(self, tok, reads, writes):
        k, v = tok
        for R in reads:
            if R.r.get(k, 0) < v:
                R.r[k] = v
        for R in writes:
            R.w = tok
            R.r = {}

    def op(self, eng, fn, reads=(), writes=()):
        waits = self._deps(eng, reads, writes)
        self.seq[eng] += 1
        tok = (eng, self.seq[eng])
        self.ops[eng].append((waits, fn, (eng, 1)))
        self._commit(tok, reads, writes)
        return tok

    def dma(self, eng, fn, reads=(), writes=(), is_out=False):
        waits = self._deps(eng, reads, writes)
        kind = "pdma" if eng == "pool" else "dma"
        n = self.NPD if kind == "pdma" else self.NDMA
        i = self.rr[kind]
        self.rr[kind] = (i + 1) % n
        key = (kind, i)
        prev = self.tot.get(key, 0)
        if prev > self.waited[eng].get(key, 0):
            self.waited[eng][key] = prev
            waits = waits + [(key, prev)]
        self.tot[key] = prev + 16
        tok = (key, self.tot[key])
        self.ops[eng].append((waits, fn, (key, 16)))
        self._commit(tok, reads, writes)
        if kind == "dma":
            self.phase_dma[key] = self.tot[key]
        if is_out:
            self.out_tokens.append(tok)
        return tok

    def cc(self, fn, reads=(), writes=(), accumulate=False):
        eng = "pool"
        waits = self._deps(eng, reads, () if accumulate else writes)
        self.tot["cc"] = self.tot.get("cc", 0) + 1
        tok = ("cc", self.tot["cc"])
        self.ops[eng].append((waits, fn, ("cc", 1)))
        self._commit(tok, reads, writes)
        return tok

    def pool_wait_all(self):
        pw = [(k, v) for k, v in self.tot.items() if (k == "cc" or k[0] == "pdma")]
        self.ops["pool"].append((pw, None, None))

    def flush(self, final=False):
        fin = dict(self.phase_dma)
        if final:
            for k, v in self.out_tokens:
                fin[k] = max(fin.get(k, 0), v)
        waits = [(k, v) for k, v in fin.items() if self.waited["sp"].get(k, 0) < v]
        for k, v in waits:
            self.waited["sp"][k] = v
        self.ops["sp"].append((waits, None, None))
        if final:
            pw = [(k, v) for k, v in self.tot.items() if (k == "cc" or k[0] == "pdma")]
            self.ops["pool"].append((pw, None, None))
        self.phase_dma = {}
        nc = self.nc
        with nc.Block() as block:
            def mk(engname):
                def body(e):
                    for waits, fn, inc in self.ops[engname]:
                        for k, v in waits:
                            e.wait_ge(self.sems[k], v)
                        if fn is not None:
                            ins = fn(e)
                            if inc is not None:
                                ins.then_inc(self.sems[inc[0]], inc[1])
                return body
            if self.ops["pe"]:
                block.tensor(mk("pe"))
            if self.ops["act"]:
                block.scalar(mk("act"))
            if self.ops["dve"]:
                block.vector(mk("dve"))
            if self.ops["pool"]:
                block.gpsimd(mk("pool"))
            block.sync(mk("sp"))
        self.ops = {k: [] for k in self.engs}


def pick_g(K, N):
    per = K // NCORES
    best = 1
    for g in range(1, min(128, per) + 1):
        if per % g == 0 and g * N * 2 <= (1 << 20):
            best = g
    return best


class Tile:
    def __init__(self, t, name=""):
        self.t = t
        self.r = Res(name)

    def __getitem__(self, idx):
        return self.t[idx]


class Rot:
    def __init__(self, tiles):
        self.tiles = tiles
        self.i = 0

    def next(self):
        t = self.tiles[self.i]
        self.i = (self.i + 1) % len(self.tiles)
        return t


def cst_layout(T):
    o = {}
    c = 0
    for name, n in [("ident", 128), ("tril", 128), ("kiota", 768), ("cmpend", 256), ("miota", 64),
                    ("eq0", 64), ("invf", 16)]:
        o[name] = (c, n)
        c += n
    o["_w"] = c
    return o


class Builder:
    def __init__(self, cfg, kinds=(0, 1, 2, 3)):
        self.c = cfg
        self.kinds = list(kinds)
        self.nc = bass.Bass("TRN2", target_bir_lowering=False)
        self.es = ExitStack()
        self.uid = 0

    def name(self, p):
        self.uid += 1
        return "%s_%d" % (p, self.uid)

    def din(self, name, shape, dt=F32):
        return self.nc.dram_tensor(name, list(shape), dt, kind="ExternalInput").ap()

    def dscr(self, name, shape, dt):
        return self.nc.dram_tensor(name, list(shape), dt)

    def tile(self, st, shape, dt, nm="t"):
        return Tile(st.enter_context(self.nc.sbuf_tensor(self.name(nm), list(shape), dt)), nm)

    def rot(self, st, n, shape, dt, nm="r"):
        return Rot([self.tile(st, shape, dt, nm) for _ in range(n)])

    def mm(self, out, lhsT, rhs, start, stop, R, W):
        self.P.op("pe", lambda e: e.matmul(out, lhsT, rhs, start=start, stop=stop), R, W)

    def tr(self, out, in_, ident, R, W):
        self.P.op("pe", lambda e: e.transpose(out, in_, ident), R, W)

    def act(self, out, in_, func, R, W, bias=None, scale=None, accum=None):
        kw = {}
        if bias is not None:
            kw["bias"] = bias
        if scale is not None:
            kw["scale"] = scale
        if accum is not None:
            kw["accum_out"] = accum
        self.P.op("act", lambda e: e.activation(out, in_, func, **kw), R, W)

    def tt(self, out, in0, in1, op, R, W, eng="dve"):
        self.P.op(eng, lambda e: e.tensor_tensor(out, in0, in1, op), R, W)

    def ts(self, out, in0, s1, s2, op0, op1, R, W, accum=None, eng="dve"):
        if op1 is None:
            self.P.op(eng, lambda e: e.tensor_scalar(out, in0, s1, None, op0), R, W)
        elif accum is None:
            self.P.op(eng, lambda e: e.tensor_scalar(out, in0, s1, s2, op0, op1), R, W)
        else:
            self.P.op(eng, lambda e: e.tensor_scalar(out, in0, s1, s2, op0, op1, accum_out=accum), R, W)

    def stt(self, out, in0, scalar, in1, op0, op1, R, W, accum=None):
        if accum is None:
            self.P.op("dve", lambda e: e.scalar_tensor_tensor(out, in0, scalar, in1, op0, op1), R, W)
        else:
            self.P.op("dve", lambda e: e.scalar_tensor_tensor(out, in0, scalar, in1, op0, op1, accum_out=accum), R, W)

    def cp(self, eng, out, in_, R, W):
        if eng == "act":
            self.P.op("act", lambda e: e.activation(out, in_, AF.Copy), R, W)
        else:
            self.P.op(eng, lambda e: e.tensor_copy(out, in_), R, W)

    def memset(self, out, val, W, eng="dve"):
        self.P.op(eng, lambda e: e.memset(out, val), (), W)

    def dma(self, out, in_, R, W, eng="sp", is_out=False, slow=False):
        if slow:
            self.P.dma(eng, lambda e: e.dma_start(out=out, in_=in_, allow_slow_non_contiguous=True), R, W, is_out)
        else:
            self.P.dma(eng, lambda e: e.dma_start(out=out, in_=in_), R, W, is_out)

    def npf(self):
        return self.pf.next()

    def npb(self):
        return self.pb.next()

    def gather_weight(self, key):
        if key in self.wfull:
            return
        src, r0, rows, N = self.wsrc[key]
        K = rows * NCORES
        if NOCC:
            full = self.dscr(self.name("wf"), [K, N], BF16)
            rf = Res("wf")
            for a in range(0, K, 128):
                b = min(K, a + 128)
                self.dma(full[a:b, :], src[r0 * NCORES + a:r0 * NCORES + b, :], [], [rf], eng="pool")
            self.wfull[key] = (full, rf)
            return
        g = pick_g(K, N)
        bounce = self.dscr(self.name("wb"), [rows, N], BF16)
        full = self.dscr(self.name("wf"), [K, N], BF16)
        rf = Res("wf")
        for sl in range(rows // g):
            rb = Res("wb")
            self.dma(bounce[sl * g:(sl + 1) * g, :], src[r0 + sl * g:r0 + (sl + 1) * g, :], [], [rb], eng="pool")
            i_ap = bounce[sl * g:(sl + 1) * g, :]
            o_ap = full[NCORES * sl * g:NCORES * (sl + 1) * g, :]
            self.P.cc(lambda e, i_ap=i_ap, o_ap=o_ap: e.collective_compute(
                "AllGather", ALU.bypass, replica_groups=[list(range(NCORES))], ins=[i_ap.opt()], outs=[o_ap.opt()]), [rb], [rf], accumulate=True)
        self.wfull[key] = (full, rf)

    def pair_gather_rows(self, bounce, rb_list, nrows, width, dt, nm, rows_per_piece=128):
        full = self.dscr(self.name(nm), [2 * nrows, width], dt)
        rf = Res(nm)
        npc = nrows // rows_per_piece
        for p in range(npc):
            i_ap = bounce[p * rows_per_piece:(p + 1) * rows_per_piece, :]
            o_ap = full[2 * p * rows_per_piece:2 * (p + 1) * rows_per_piece, :]
            rb = rb_list[p] if isinstance(rb_list, list) else rb_list
            if NOCC:
                for hh in range(2):
                    for q0 in range(0, rows_per_piece, 128):
                        self.dma(full[(2 * p + hh) * rows_per_piece + q0:(2 * p + hh) * rows_per_piece + q0 + 128, :],
                                 bounce[p * rows_per_piece + q0:p * rows_per_piece + q0 + 128, :], [rb], [rf], eng="pool")
                continue
            self.P.cc(lambda e, i_ap=i_ap, o_ap=o_ap: e.collective_compute(
                "AllGather", ALU.bypass, replica_groups=[[0, 1], [2, 3], [4, 5], [6, 7]], ins=[i_ap.opt()], outs=[o_ap.opt()]), [rb], [rf], accumulate=True)
        return full, rf

    def grow(self, gb):
        return gb * 128

    def rstd_from_ss(self, st_small, ss_ap, n, width, R, W_tile):
        self.act(W_tile[:, 0:n], ss_ap, AF.Sqrt, R, [W_tile.r], bias=self.epsb[:, 0:1], scale=1.0 / width)
        self.P.op("dve", lambda e: e.reciprocal(W_tile[:, 0:n], W_tile[:, 0:n]), [W_tile.r], [W_tile.r])

    def norm_transpose(self, src_ap, src_res, gain_bc, xT, col0, bufs):
        c = self.c
        D, KC = c["D"], c["KC"]
        xin, xn, sm = bufs["xin"].next(), bufs["xn"].next(), bufs["sm"].next()
        self.dma(xin[:, :], src_ap, src_res, [xin.r])
        self.act(xn[:, :], xin[:, :], AF.Square, [xin.r], [xn.r, sm.r], accum=sm[:, 0:1])
        self.rstd_from_ss(None, sm[:, 0:1], 1, D, [sm.r], sm)
        self.stt(xn[:, :], xin[:, :], sm[:, 0:1], gain_bc[:, :], ALU.mult, ALU.mult, [xin.r, sm.r, gain_bc.r], [xn.r])
        self.transpose_into(xn, D, xT, col0)

    def transpose_into(self, src, ncols, dst, col0, dst_k0=0, rows=128):
        nk = ncols // 128
        k = 0
        flip = 0
        while k < nk:
            g = min(8, nk - k)
            pb = self.npb()
            for q in range(g):
                self.tr(pb[:, q * 128:q * 128 + rows], src[0:rows, (k + q) * 128:(k + q + 1) * 128],
                        self.identb[0:rows, 0:rows], [src.r], [pb.r])
            outv = dst[:, dst_k0 + k:dst_k0 + k + g, col0:col0 + rows]
            inv = pb[:, 0:g * 128].rearrange("p (a b) -> p a b", b=128)[:, :, 0:rows]
            self.cp("act" if flip else "dve", outv, inv, [pb.r], [dst.r])
            flip ^= 1
            k += g

    def xr_res(self, gs, n):
        key = (gs, n)
        if key not in self.xres_r:
            self.xres_r[key] = Res("x")
        return self.xres_r[key]

    def x_all_res(self, gs):
        return [self.xr_res(gs, n) for n in range(self.c["D"] // self.CWD)]

    def down_proj(self, st, actT, FCn, wkey, scale, tt, TT, dst, dst_is_out):
        c = self.c
        D = c["D"]
        CWD = self.CWD
        full, rf = self.wfull[wkey]
        wv = full.ap().rearrange("(f p) n -> p f n", p=128)
        for n in range(D // CWD):
            wt = self.wrot.next()
            self.dma(wt[:, 0:FCn * CWD].rearrange("p (f n) -> p f n", n=CWD), wv[:, :, n * CWD:(n + 1) * CWD], [rf], [wt.r])
            wtv = wt[:, 0:FCn * CWD].rearrange("p (f n) -> p f n", n=CWD)
            for s in range(TT // 128):
                gs = tt * (TT // 128) + s
                ps = self.npf()
                for f in range(FCn):
                    self.mm(ps[:, 0:CWD], actT[:, f, s * 128:(s + 1) * 128], wtv[:, f, :], f == 0, f == FCn - 1,
                            [actT.r, wt.r], [ps.r])
                xr = self.evrot.next()
                rr = self.xr_res(gs, n)
                self.dma(xr[:, 0:CWD], self.xres[gs * 128:(gs + 1) * 128, n * CWD:(n + 1) * CWD], [rr], [xr.r])
                self.stt(xr[:, 0:CWD], ps[:, 0:CWD], float(scale), xr[:, 0:CWD], ALU.mult, ALU.add, [ps.r, xr.r], [xr.r])
                if dst_is_out:
                    self.dma(self.out[gs * 128:(gs + 1) * 128, n * CWD:(n + 1) * CWD], xr[:, 0:CWD], [xr.r], [Res()], is_out=True)
                else:
                    self.dma(self.xres[gs * 128:(gs + 1) * 128, n * CWD:(n + 1) * CWD], xr[:, 0:CWD], [xr.r], [rr])

    def load_T(self, st, dst_ap, dst_res, src_ap, r):
        tmp = self.tile(st, [128, 128], F32, "ltmp")
        self.dma(tmp[0:r, :], src_ap, [], [tmp.r])
        ps = self.npf()
        self.tr(ps[:, 0:r], tmp[0:r, :], self.identf[0:r, 0:r], [tmp.r], [ps.r])
        self.cp("dve", dst_ap, ps[:, 0:r], [ps.r], [dst_res])

    def load_bc(self, st, vec_ap, n, nm):
        t = self.tile(st, [128, n], F32, nm)
        self.dma(t[:, :], vec_ap.partition_broadcast(128), [], [t.r])
        return t

    def phase_ffn(self, layer, which, is_last):
        c = self.c
        D, KC, FC, DFF, TL = c["D"], c["KC"], c["FC"], c["DFF"], c["TL"]
        TT = min(512, TL)
        with ExitStack() as st:
            gain = self.load_bc(st, self.inp["ffn%d_norm" % which][layer, :], D, "gain")
            bufs = dict(xin=self.rot(st, 1, [128, D], F32, "xin"), xn=self.rot(st, 1, [128, D], BF16, "xn"),
                        sm=self.rot(st, 2, [128, 8], F32, "sm"))
            xT = self.tile(st, [128, KC, TT], BF16, "xT")
            actT = self.tile(st, [128, FC, TT], BF16, "actT")
            self.wrot = self.rot(st, 2, [128, max(KC * 512, FC * self.CWD)], BF16, "w")
            self.evrot = self.rot(st, 3, [128, 512], F32, "ev")
            wgu, rgu = self.wfull[("gu", which, layer)]
            wguv = wgu.ap().rearrange("(k p) n -> p k n", p=128)
            for tt in range(TL // TT):
                for s in range(TT // 128):
                    gs = tt * (TT // 128) + s
                    self.norm_transpose(self.xres[gs * 128:(gs + 1) * 128, :], self.x_all_res(gs), gain, xT, s * 128, bufs)
                wt = wv = None
                w = 256
                for fc in range(FC):
                    if fc % 2 == 0:
                        w = min(256, DFF - fc * 128)
                        wt = self.wrot.next()
                        wv = wt[:, 0:KC * 2 * w].rearrange("p (k n) -> p k n", n=2 * w)
                        self.dma(wv[:, :, 0:w], wguv[:, :, fc * 128:fc * 128 + w], [rgu], [wt.r])
                        self.dma(wv[:, :, w:2 * w], wguv[:, :, DFF + fc * 128:DFF + fc * 128 + w], [rgu], [wt.r])
                    o = (fc % 2) * 128
                    pg, pu = self.npf(), self.npf()
                    for k in range(KC):
                        self.mm(pg[:, 0:TT], wv[:, k, o:o + 128], xT[:, k, :], k == 0, k == KC - 1, [wt.r, xT.r], [pg.r])
                    for k in range(KC):
                        self.mm(pu[:, 0:TT], wv[:, k, w + o:w + o + 128], xT[:, k, :], k == 0, k == KC - 1, [wt.r, xT.r], [pu.r])
                    sg = self.evrot.next()
                    self.act(sg[:, 0:TT], pg[:, 0:TT], AF.Silu, [pg.r], [sg.r])
                    self.tt(actT[:, fc, :], sg[:, 0:TT], pu[:, 0:TT], ALU.mult, [sg.r, pu.r], [actT.r])
                self.down_proj(st, actT, FC, ("down", which, layer), 0.5, tt, TT, None, is_last)
            self.P.flush()

    def phase_inproj(self, layer):
        c = self.c
        D, KC, TL = c["D"], c["KC"], c["TL"]
        kind = self.kinds[layer]
        NIN = c["NIN"][kind]
        TT = min(512, TL)
        self.proj = self.dscr(self.name("proj"), [TL, NIN], F32)
        self.proj_r = [Res("proj") for _ in range(TL // 128)]
        with ExitStack() as st:
            gain = self.load_bc(st, self.inp["mix_norm"][layer, :], D, "gain")
            bufs = dict(xin=self.rot(st, 1, [128, D], F32, "xin"), xn=self.rot(st, 1, [128, D], BF16, "xn"),
                        sm=self.rot(st, 2, [128, 8], F32, "sm"))
            xT = self.tile(st, [128, KC, TT], BF16, "xT")
            wrot = self.rot(st, 2, [128, KC * 512], BF16, "w")
            evrot = self.rot(st, 3, [128, 512], F32, "ev")
            bias = None
            if kind == 1:
                bias = self.load_bc(st, self.inp["conv_b_in"][0, :], 2 * D, "bin")
            win, rin = self.wfull[("in", layer)]
            winv = win.ap().rearrange("(k p) n -> p k n", p=128)
            for tt in range(TL // TT):
                for s in range(TT // 128):
                    gs = tt * (TT // 128) + s
                    self.norm_transpose(self.xres[gs * 128:(gs + 1) * 128, :], self.x_all_res(gs), gain, xT, s * 128, bufs)
                for n0 in range(0, NIN, 512):
                    w = min(512, NIN - n0)
                    wt = wrot.next()
                    self.dma(wt[:, 0:KC * w].rearrange("p (k n) -> p k n", n=w), winv[:, :, n0:n0 + w], [rin], [wt.r])
                    wtv = wt[:, 0:KC * w].rearrange("p (k n) -> p k n", n=w)
                    for s in range(TT // 128):
                        gs = tt * (TT // 128) + s
                        ps = self.npf()
                        for k in range(KC):
                            self.mm(ps[:, 0:w], xT[:, k, s * 128:(s + 1) * 128], wtv[:, k, :], k == 0, k == KC - 1, [xT.r, wt.r], [ps.r])
                        ev = evrot.next()
                        if bias is not None and n0 < 2 * D:
                            self.tt(ev[:, 0:w], ps[:, 0:w], bias[:, n0:n0 + w], ALU.add, [ps.r, bias.r], [ev.r])
                        else:
                            self.cp("act", ev[:, 0:w], ps[:, 0:w], [ps.r], [ev.r])
                        self.dma(self.proj[gs * 128:(gs + 1) * 128, n0:n0 + w], ev[:, 0:w], [ev.r], [self.proj_r[gs]])
            self.P.flush()

    def phase_memkv(self):
        c = self.c
        D, KC, MEMW, MH, MHD, MDC = c["D"], c["KC"], c["MEMW"], c["MH"], c["MHD"], c["MDC"]
        with ExitStack() as st:
            gain = self.load_bc(st, self.inp["mem_norm"][:], D, "gain")
            kg = self.load_bc(st, self.inp["mem_k_norm"][:], MHD, "kg")
            bufs = dict(xin=self.rot(st, 1, [128, D], F32, "xin"), xn=self.rot(st, 1, [128, D], BF16, "xn"),
                        sm=self.rot(st, 2, [128, 8], F32, "sm"))
            memT = self.tile(st, [128, KC, 256], BF16, "memT")
            wrot = self.rot(st, 2, [128, KC * 512], BF16, "w")
            kv = self.tile(st, [128, 2 * MEMW], F32, "kv")
            sq = self.tile(st, [128, MEMW], F32, "sq")
            kb = self.tile(st, [128, MEMW], BF16, "kb")
            for s in range(2):
                self.norm_transpose(self.inp["mem"][s * 128:(s + 1) * 128, :], [], gain, memT, s * 128, bufs)
            wkv, rkv = self.wfull[("memkv",)]
            wv = wkv.ap().rearrange("(k p) n -> p k n", p=128)
            for s in range(2):
                for n0 in range(0, 2 * MEMW, 512):
                    w = min(512, 2 * MEMW - n0)
                    wt = wrot.next()
                    self.dma(wt[:, 0:KC * w].rearrange("p (k n) -> p k n", n=w), wv[:, :, n0:n0 + w], [rkv], [wt.r])
                    wtv = wt[:, 0:KC * w].rearrange("p (k n) -> p k n", n=w)
                    ps = self.npf()
                    for k in range(KC):
                        self.mm(ps[:, 0:w], memT[:, k, s * 128:(s + 1) * 128], wtv[:, k, :], k == 0, k == KC - 1, [memT.r, wt.r], [ps.r])
                    self.cp("act", kv[:, n0:n0 + w], ps[:, 0:w], [ps.r], [kv.r])
                sm = bufs["sm"].next()
                self.tt(sq[:, :], kv[:, 0:MEMW], kv[:, 0:MEMW], ALU.mult, [kv.r], [sq.r])
                self.P.op("dve", lambda e, sm=sm: e.tensor_reduce(sm[:, 0:MH], sq[:, :].rearrange("p (h d) -> p h d", d=MHD), AX.X, ALU.add), [sq.r], [sm.r])
                self.rstd_from_ss(None, sm[:, 0:MH], MH, MHD, [sm.r], sm)
                self.tt(sq[:, :].rearrange("p (h d) -> p h d", d=MHD), kv[:, 0:MEMW].rearrange("p (h d) -> p h d", d=MHD),
                        sm[:, 0:MH].unsqueeze(2).broadcast_to([128, MH, MHD]), ALU.mult, [kv.r, sm.r], [sq.r])
                self.tt(kb[:, :].rearrange("p (h d) -> p h d", d=MHD), sq[:, :].rearrange("p (h d) -> p h d", d=MHD),
                        kg[:, :].unsqueeze(1).broadcast_to([128, MH, MHD]), ALU.mult, [sq.r, kg.r], [kb.r])
                self.transpose_into(kb, MEMW, self.mkT, s * 128)
                self.cp("dve", self.mv[:, s, :], kv[:, MEMW:2 * MEMW], [kv.r], [self.mv.r])
            self.P.flush()

    def phase_memattn(self, layer):
        c = self.c
        D, TL, MEMW, MH, MHD, MDC = c["D"], c["TL"], c["MEMW"], c["MH"], c["MHD"], c["MDC"]
        NIN = c["NIN"][self.kinds[layer]]
        with ExitStack() as st:
            qg = self.load_bc(st, self.inp["mem_q_norm"][layer, :], MHD, "qg")
            mqr = self.rot(st, 2, [128, MEMW], F32, "mq")
            sq = self.tile(st, [128, MEMW], F32, "sq")
            qb = self.tile(st, [128, MEMW], BF16, "qb")
            qT = self.tile(st, [128, MH * MDC, 128], BF16, "qT")
            smr = self.rot(st, 4, [128, 8], F32, "sm")
            pr = self.rot(st, 2, [128, 256], BF16, "p")
            pT = self.tile(st, [128, 2, 128], BF16, "pT")
            orot = self.rot(st, 2, [128, MEMW], BF16, "o")
            for gs in range(TL // 128):
                mq = mqr.next()
                self.dma(mq[:, :], self.proj[gs * 128:(gs + 1) * 128, NIN - MEMW:NIN], [self.proj_r[gs]], [mq.r])
                sm = smr.next()
                self.tt(sq[:, :], mq[:, :], mq[:, :], ALU.mult, [mq.r], [sq.r])
                self.P.op("dve", lambda e, sm=sm: e.tensor_reduce(sm[:, 0:MH], sq[:, :].rearrange("p (h d) -> p h d", d=MHD), AX.X, ALU.add), [sq.r], [sm.r])
                self.rstd_from_ss(None, sm[:, 0:MH], MH, MHD, [sm.r], sm)
                self.tt(sq[:, :].rearrange("p (h d) -> p h d", d=MHD), mq[:, :].rearrange("p (h d) -> p h d", d=MHD),
                        sm[:, 0:MH].unsqueeze(2).broadcast_to([128, MH, MHD]), ALU.mult, [mq.r, sm.r], [sq.r])
                self.tt(qb[:, :].rearrange("p (h d) -> p h d", d=MHD), sq[:, :].rearrange("p (h d) -> p h d", d=MHD),
                        qg[:, :].unsqueeze(1).broadcast_to([128, MH, MHD]), ALU.mult, [sq.r, qg.r], [qb.r])
                self.transpose_into(qb, MEMW, qT, 0)
                ot = orot.next()
                for h in range(MH):
                    ps = self.npf()
                    for dc in range(MDC):
                        self.mm(ps[:, 0:256], qT[:, h * MDC + dc, :], self.mkT[:, h * MDC + dc, :], dc == 0, dc == MDC - 1,
                                [qT.r, self.mkT.r], [ps.r])
                    p = pr.next()
                    rs = smr.next()
                    self.act(p[:, :], ps[:, 0:256], AF.Exp, [ps.r], [p.r, rs.r], scale=float(MHD) ** -0.5, accum=rs[:, 0:1])
                    self.P.op("dve", lambda e, rs=rs: e.reciprocal(rs[:, 1:2], rs[:, 0:1]), [rs.r], [rs.r])
                    self.transpose_into(p, 256, pT, 0)
                    po = self.npf()
                    for mc in range(2):
                        self.mm(po[:, 0:MHD], pT[:, mc, :], self.mv[:, mc, h * MHD:(h + 1) * MHD], mc == 0, mc == 1,
                                [pT.r, self.mv.r], [po.r])
                    self.ts(ot[:, h * MHD:(h + 1) * MHD], po[:, 0:MHD], rs[:, 1:2], None, ALU.mult, None, [po.r, rs.r], [ot.r])
                self.dma(self.cat[gs * 128:(gs + 1) * 128, D:D + MEMW], ot[:, :], [ot.r], [self.cat_r[gs]])
            self.P.flush()

    def phase_outproj(self, layer):
        c = self.c
        D, TL, OUTIN = c["D"], c["TL"], c["OUTIN"]
        OC = OUTIN // 128
        TT = min(512, TL)
        with ExitStack() as st:
            catT = self.tile(st, [128, OC, TT], BF16, "catT")
            cin = self.rot(st, 2, [128, OUTIN], BF16, "cin")
            self.wrot = self.rot(st, 2, [128, OC * self.CWD], BF16, "w")
            self.evrot = self.rot(st, 3, [128, 512], F32, "ev")
            for tt in range(TL // TT):
                for s in range(TT // 128):
                    gs = tt * (TT // 128) + s
                    ci = cin.next()
                    self.dma(ci[:, :], self.cat[gs * 128:(gs + 1) * 128, :], [self.cat_r[gs]], [ci.r])
                    self.transpose_into(ci, OUTIN, catT, s * 128)
                self.down_proj(st, catT, OC, ("out", layer), 1.0, tt, TT, None, False)
            self.P.flush()

    def phase_gmlp(self):
        c = self.c
        D, TL, NH = c["D"], c["TL"], c["NH"]
        with ExitStack() as st:
            lg = self.load_bc(st, self.inp["gm_ln_g"][0, :], D, "lg")
            lb = self.load_bc(st, self.inp["gm_ln_b"][0, :], D, "lb")
            wsf = self.tile(st, [128, NH, 128], F32, "wsf")
            wsb = self.tile(st, [128, NH * 128], BF16, "wsb")
            wsT = self.tile(st, [128, NH, 128], BF16, "wsT")
            bsT = self.tile(st, [128, NH], F32, "bsT")
            self.dma(wsf[:, :, :], self.inp["gm_ws"][0].rearrange("g t s -> t g s"), [], [wsf.r])
            self.load_T(st, bsT[:, :], bsT.r, self.inp["gm_bs"][0], NH)
            self.tt(wsb[:, :].rearrange("p (g s) -> p g s", s=128), wsf[:, :, :],
                    self.tril[:, :].unsqueeze(1).broadcast_to([128, NH, 128]), ALU.mult, [wsf.r, self.cst.r], [wsb.r])
            self.transpose_into(wsb, NH * 128, wsT, 0)
            pin = self.rot(st, 1, [128, 2 * D], F32, "pin")
            vb = self.tile(st, [128, D], BF16, "vb")
            vn = self.tile(st, [128, D], F32, "vn")
            stats = self.tile(st, [128, (D // 512) * 6], F32, "stats")
            smr = self.rot(st, 2, [128, 8], F32, "sm")
            orot = self.rot(st, 2, [128, D], BF16, "o")
            tmp = self.rot(st, 2, [128, 512], F32, "tmp")
            for gs in range(TL // 128):
                p = pin.next()
                self.dma(p[:, :], self.proj[gs * 128:(gs + 1) * 128, 0:2 * D], [self.proj_r[gs]], [p.r])
                self.act(p[:, :], p[:, :], AF.Gelu, [p.r], [p.r])
                for q in range(D // 512):
                    self.P.op("dve", lambda e, q=q, p=p: e.bn_stats(stats[:, q * 6:(q + 1) * 6], p[:, D + q * 512:D + (q + 1) * 512]), [p.r], [stats.r])
                sm = smr.next()
                self.P.op("dve", lambda e, sm=sm: e.bn_aggr(sm[:, 0:2], stats[:, :]), [stats.r], [sm.r])
                self.act(sm[:, 2:3], sm[:, 1:2], AF.Sqrt, [sm.r], [sm.r], bias=self.epsb[:, 0:1], scale=1.0)
                self.P.op("dve", lambda e, sm=sm: e.reciprocal(sm[:, 2:3], sm[:, 2:3]), [sm.r], [sm.r])
                self.ts(vn[:, :], p[:, D:2 * D], sm[:, 0:1], sm[:, 2:3], ALU.subtract, ALU.mult, [p.r, sm.r], [vn.r])
                self.tt(vn[:, :], vn[:, :], lg[:, :], ALU.mult, [vn.r, lg.r], [vn.r])
                self.tt(vb[:, :], vn[:, :], lb[:, :], ALU.add, [vn.r, lb.r], [vb.r])
                ot = orot.next()
                for g0 in range(0, NH, 4):
                    ps = self.npf()
                    for q in range(4):
                        gi = g0 + q
                        self.mm(ps[:, q * 128:(q + 1) * 128], wsT[:, gi, :], vb[:, gi * 128:(gi + 1) * 128], True, True, [wsT.r, vb.r], [ps.r])
                    t = tmp.next()
                    self.tt(t[:, :].rearrange("p (g c) -> p g c", c=128), ps[:, :].rearrange("p (g c) -> p g c", c=128),
                            bsT[:, g0:g0 + 4].unsqueeze(2).broadcast_to([128, 4, 128]), ALU.add, [ps.r, bsT.r], [t.r])
                    self.tt(ot[:, g0 * 128:(g0 + 4) * 128], t[:, :], p[:, g0 * 128:(g0 + 4) * 128], ALU.mult, [t.r, p.r], [ot.r])
                self.dma(self.cat[gs * 128:(gs + 1) * 128, 0:D], ot[:, :], [ot.r], [self.cat_r[gs]])
            self.P.flush()

    def phase_conv(self):
        c = self.c
        D, TL, NB, KC = c["D"], c["TL"], c["NB"], c["KC"]
        uT_d = self.dscr(self.name("uT"), [D, TL], BF16)
        uT_r = Res("uT")
        tails = self.dscr(self.name("tails"), [D, NB * 32], BF16)
        tails_r = Res("tails")
        y_d = self.dscr(self.name("yd"), [TL, D], F32)
        y_r = [Res("y") for _ in range(NB)]
        with ExitStack() as st:
            pin = self.rot(st, 2, [128, 2 * D], F32, "pin")
            ub = self.rot(st, 2, [128, D], BF16, "ub")
            uT = self.rot(st, 2, [128, KC, 128], BF16, "uTt")
            for gs in range(NB):
                p = pin.next()
                self.dma(p[:, :], self.proj[gs * 128:(gs + 1) * 128, 0:2 * D], [self.proj_r[gs]], [p.r])
                self.act(p[:, D:2 * D], p[:, D:2 * D], AF.Sigmoid, [p.r], [p.r])
                u = ub.next()
                self.tt(u[:, :], p[:, 0:D], p[:, D:2 * D], ALU.mult, [p.r], [u.r])
                ut = uT.next()
                self.transpose_into(u, D, ut, 0)
                self.dma(uT_d.ap().rearrange("(k p) t -> p k t", p=128)[:, :, gs * 128:(gs + 1) * 128], ut[:, :, :], [ut.r], [uT_r])
                self.dma(tails.ap().rearrange("(k p) t -> p k t", p=128)[:, :, gs * 32:(gs + 1) * 32], ut[:, :, 96:128], [ut.r], [tails_r])
            self.P.flush()
        TP = min(1024, D)
        tg, tg_r = self.pair_gather_rows(tails, tails_r, D, NB * 32, BF16, "tg", rows_per_piece=TP)
        self.P.pool_wait_all()
        self.P.flush()
        with ExitStack() as st:
            wtm = self.tile(st, [32, D], F32, "wtm")
            self.dma(wtm[0:31, :], self.inp["conv_dw_w"][0], [], [wtm.r])
            wT = self.tile(st, [128, KC, 32], F32, "wT")
            dwb = self.tile(st, [128, KC], F32, "dwb")
            self.load_T(st, dwb[:, :], dwb.r, self.inp["conv_dw_b"][0].rearrange("(k p) -> k p", p=128), KC)
            for k in range(KC):
                ps = self.npf()
                self.tr(ps[:, 0:31], wtm[0:31, k * 128:(k + 1) * 128], self.identf[0:31, 0:31], [wtm.r], [ps.r])
                self.cp("dve", wT[:, k, 0:31], ps[:, 0:31], [ps.r], [wT.r])
            up = self.rot(st, 2, [128, NB, 160], BF16, "up")
            ca = self.rot(st, 2, [128, NB, 32], BF16, "ca")
            cb = self.rot(st, 2, [128, NB, 32], BF16, "cb")
            acc = self.rot(st, 2, [128, NB, 128], F32, "acc")
            yt = self.rot(st, 2, [128, 512], F32, "yt")
            tgv = tg.ap()
            uTv = uT_d.ap()
            for k in range(KC):
                u_, a_, b_ = up.next(), ca.next(), cb.next()
                self.dma(u_[:, :, 32:160], uTv[k * 128:(k + 1) * 128, :].rearrange("p (j t) -> p j t", t=128), [uT_r], [u_.r])
                self.memset(a_[:, 0, :], 0.0, [a_.r])
                if NB > 1:
                    ra = ((k * 128) // TP * 2 + 1) * TP + (k * 128) % TP
                    self.dma(a_[:, 1:NB, :], tgv[ra:ra + 128, 0:(NB - 1) * 32].rearrange("p (j t) -> p j t", t=32), [tg_r], [a_.r])
                rb0 = ((k * 128) // TP * 2) * TP + (k * 128) % TP
                self.dma(b_[:, :, :], tgv[rb0:rb0 + 128, :].rearrange("p (j t) -> p j t", t=32), [tg_r], [b_.r])
                self.tt(b_[:, :, :], b_[:, :, :], a_[:, :, :], ALU.subtract, [a_.r, b_.r], [b_.r])
                self.stt(u_[:, :, 0:32], b_[:, :, :], self.hf[:, 0:1], a_[:, :, :], ALU.mult, ALU.add, [a_.r, b_.r, self.hf.r], [u_.r])
                ac = acc.next()
                self.ts(ac[:, :, :], u_[:, :, 2:130], wT[:, k, 0:1], dwb[:, k:k + 1], ALU.mult, ALU.add, [u_.r, wT.r, dwb.r], [ac.r])
                for tap in range(1, 31):
                    self.stt(ac[:, :, :], u_[:, :, 2 + tap:130 + tap], wT[:, k, tap:tap + 1], ac[:, :, :], ALU.mult, ALU.add,
                             [u_.r, wT.r, ac.r], [ac.r])
                for j0 in range(0, NB, 4):
                    nj = min(4, NB - j0)
                    ps = self.npf()
                    for q in range(nj):
                        self.tr(ps[:, q * 128:(q + 1) * 128], ac[:, j0 + q, :], self.identf[:, :], [ac.r], [ps.r])
                    y = yt.next()
                    self.cp("act", y[:, 0:nj * 128], ps[:, 0:nj * 128], [ps.r], [y.r])
                    for q in range(nj):
                        self.dma(y_d[(j0 + q) * 128:(j0 + q + 1) * 128, k * 128:(k + 1) * 128], y[:, q * 128:(q + 1) * 128], [y.r], [y_r[j0 + q]])
            self.P.flush()
        with ExitStack() as st:
            lg = self.load_bc(st, self.inp["conv_ln_g"][0, :], D, "lg")
            lb = self.load_bc(st, self.inp["conv_ln_b"][0, :], D, "lb")
            yin = self.rot(st, 2, [128, D], F32, "yin")
            stats = self.tile(st, [128, (D // 512) * 6], F32, "stats")
            smr = self.rot(st, 2, [128, 8], F32, "sm")
            orot = self.rot(st, 2, [128, D], BF16, "o")
            for gs in range(NB):
                y = yin.next()
                self.dma(y[:, :], y_d[gs * 128:(gs + 1) * 128, :], [y_r[gs]], [y.r])
                for q in range(D // 512):
                    self.P.op("dve", lambda e, q=q, y=y: e.bn_stats(stats[:, q * 6:(q + 1) * 6], y[:, q * 512:(q + 1) * 512]), [y.r], [stats.r])
                sm = smr.next()
                self.P.op("dve", lambda e, sm=sm: e.bn_aggr(sm[:, 0:2], stats[:, :]), [stats.r], [sm.r])
                self.act(sm[:, 2:3], sm[:, 1:2], AF.Sqrt, [sm.r], [sm.r], bias=self.epsb[:, 0:1], scale=1.0)
                self.P.op("dve", lambda e, sm=sm: e.reciprocal(sm[:, 2:3], sm[:, 2:3]), [sm.r], [sm.r])
                self.ts(y[:, :], y[:, :], sm[:, 0:1], sm[:, 2:3], ALU.subtract, ALU.mult, [y.r, sm.r], [y.r])
                self.tt(y[:, :], y[:, :], lg[:, :], ALU.mult, [y.r, lg.r], [y.r])
                self.tt(y[:, :], y[:, :], lb[:, :], ALU.add, [y.r, lb.r], [y.r])
                ot = orot.next()
                self.act(ot[:, :], y[:, :], AF.Silu, [y.r], [ot.r])
                self.dma(self.cat[gs * 128:(gs + 1) * 128, 0:D], ot[:, :], [ot.r], [self.cat_r[gs]])
            self.P.flush()

    def load_gathered(self, dst, full, rf, c0, w):
        NB = self.c["NB"]
        src = full.ap()[:, c0:c0 + w].rearrange("(b i) c -> i b c", i=128)
        step = max(1, NB // 2)
        for b0 in range(0, 2 * NB, step):
            self.dma(dst[:, b0:b0 + step, 0:w], src[:, b0:b0 + step, :], [rf], [dst.r])

    def phase_sb(self):
        c = self.c
        D, TL, NB, NH, T = c["D"], c["TL"], c["NB"], c["NH"], c["T"]
        scale = 128.0 ** -0.5
        gath = []
        for part in range(2):
            kvb = self.dscr(self.name("kvb"), [TL, D], BF16)
            rbs = [Res("kvb") for _ in range(NB)]
            for gs in range(NB):
                self.dma(kvb[gs * 128:(gs + 1) * 128, :], self.proj[gs * 128:(gs + 1) * 128, (1 + part) * D:(2 + part) * D],
                         [self.proj_r[gs]], [rbs[gs]], eng="pool")
            gath.append(self.pair_gather_rows(kvb, rbs, TL, D, BF16, "kvg"))
        (kg_, kg_r), (vg_, vg_r) = gath
        self.P.pool_wait_all()
        self.P.flush()
        with ExitStack() as st:
            kt = self.rot(st, 1, [128, 2 * NB, 512], BF16, "kt")
            vt = self.rot(st, 1, [128, 2 * NB, 512], BF16, "vt")
            cms = self.tile(st, [128, 256], F32, "cms")
            kT = self.tile(st, [128, 4, T], BF16, "kT")
            qin = self.rot(st, 2, [128, 512], F32, "qin")
            qb = self.rot(st, 2, [128, 512], BF16, "qb")
            qT = self.rot(st, 2, [128, 4, 128], BF16, "qT")
            Z = self.tile(st, [128, T], F32, "Z")
            E = self.tile(st, [128, T], F32, "E")
            C = self.tile(st, [128, T], F32, "C")
            Aw = self.tile(st, [128, T], BF16, "Aw")
            AwT = self.tile(st, [128, 2 * NB, 128], BF16, "AwT")
            smr = self.rot(st, 4, [128, 8], F32, "sm")
            orot = self.rot(st, 2, [128, 512], BF16, "o")
            for hg in range(NH // 4):
                k_, v_ = kt.next(), vt.next()
                self.load_gathered(k_, kg_, kg_r, hg * 512, 512)
                self.load_gathered(v_, vg_, vg_r, hg * 512, 512)
                for gb in range(2 * NB):
                    pb = self.npb()
                    for hh in range(4):
                        self.tr(pb[:, hh * 128:(hh + 1) * 128], k_[:, gb, hh * 128:(hh + 1) * 128], self.identb[:, :], [k_.r], [pb.r])
                    self.cp("act" if gb % 2 else "dve", kT[:, :, gb * 128:(gb + 1) * 128],
                            pb[:, 0:512].rearrange("p (a b) -> p a b", b=128), [pb.r], [kT.r])
                ksb = int(_os.environ.get("KSB", "9"))
                for j in range(NB):
                    if ksb < 1:
                        break
                    nkb = 2 * j + 2
                    nk = nkb * 128
                    qi, qbb, qt = qin.next(), qb.next(), qT.next()
                    self.dma(qi[:, :], self.proj[j * 128:(j + 1) * 128, hg * 512:(hg + 1) * 512], [self.proj_r[j]], [qi.r])
                    self.cp("dve", qbb[:, :], qi[:, :], [qi.r], [qbb.r])
                    self.transpose_into(qbb, 512, qt, 0)
                    ot = orot.next()
                    smq = smr.next()
                    self.ts(smq[:, 4:5], self.qpos[:, j:j + 1], float(-2 * j * 128), None, ALU.add, None, [self.qpos.r], [smq.r])
                    self.ts(cms[:, :], self.kio[:, 0:256], smq[:, 4:5], None, ALU.is_lt, None, [self.cst.r, smq.r], [cms.r])
                    for hh in range(4):
                        for k0 in range(0, nk, 512):
                            w = min(512, nk - k0)
                            ps = self.npf()
                            self.mm(ps[:, 0:w], qt[:, hh, :], kT[:, hh, k0:k0 + w], True, True, [qt.r, kT.r], [ps.r])
                            self.ts(Z[:, k0:k0 + w], ps[:, 0:w], scale, None, ALU.mult, None, [ps.r], [Z.r])
                            self.act(E[:, k0:k0 + w], Z[:, k0:k0 + w], AF.Exp, [Z.r], [E.r])
                        if ksb < 2:
                            continue
                        self.act(E[:, 0:nk], E[:, 0:nk], AF.Ln, [E.r], [E.r], bias=self.oneb[:, 0:1], scale=1.0)
                        if ksb < 3:
                            continue
                        self.tt(E[:, nk - 256:nk], E[:, nk - 256:nk], cms[:, :], ALU.mult, [E.r, cms.r], [E.r])
                        self.P.op("dve", lambda e, nk=nk: e.tensor_tensor_scan(C[:, 0:nk], self.oneb[:, 0:1].broadcast_to([128, nk]), E[:, 0:nk], 0.0, ALU.mult, ALU.add),
                                  [E.r, self.oneb.r], [C.r])
                        if ksb < 4:
                            continue
                        sm = smr.next()
                        self.ts(sm[:, 0:1], C[:, nk - 1:nk], -1.0, None, ALU.mult, None, [C.r], [sm.r])
                        self.tt(Z[:, 1:nk], Z[:, 1:nk], C[:, 0:nk - 1], ALU.add, [Z.r, C.r], [Z.r])
                        self.act(Aw[:, 0:nk], Z[:, 0:nk], AF.Exp, [Z.r, sm.r], [Aw.r], bias=sm[:, 0:1], scale=1.0)
                        self.tt(Aw[:, nk - 256:nk], Aw[:, nk - 256:nk], cms[:, :], ALU.mult, [Aw.r, cms.r], [Aw.r])
                        if ksb < 5:
                            continue
                        self.transpose_into(Aw, nk, AwT, 0)
                        po = self.npf()
                        for kb in range(nkb):
                            self.mm(po[:, 0:128], AwT[:, kb, :], v_[:, kb, hh * 128:(hh + 1) * 128], kb == 0, kb == nkb - 1,
                                    [AwT.r, v_.r], [po.r])
                        self.cp("act", ot[:, hh * 128:(hh + 1) * 128], po[:, 0:128], [po.r], [ot.r])
                    self.dma(self.cat[j * 128:(j + 1) * 128, hg * 512:(hg + 1) * 512], ot[:, :], [ot.r], [self.cat_r[j]])
            self.P.flush()

    def phase_nsa(self):
        c = self.c
        D, TL, NB, NH, T, G, HPG, NCMP = c["D"], c["TL"], c["NB"], c["NH"], c["T"], c["G"], c["HPG"], c["NCMP"]
        NSEL = c["NSEL"]
        scale = 128.0 ** -0.5
        qn_d = self.dscr(self.name("qn"), [TL, D], BF16)
        qn_r = [Res("qn") for _ in range(NB)]
        gt_d = self.dscr(self.name("gt"), [TL, 3 * NH], F32)
        gt_r = [Res("gt") for _ in range(NB)]
        kvb = self.dscr(self.name("nkvb"), [TL, 3072], BF16)
        kvb_rs = [Res("nkvb") for _ in range(NB)]
        with ExitStack() as st:
            qg = self.load_bc(st, self.inp["nsa_q_norm"][0, :], 128, "qg")
            kg = [self.load_bc(st, self.inp["nsa_k_norm"][0, i, :], 128, "kg") for i in range(3)]
            pin = self.rot(st, 1, [128, c["NSA_IN"]], F32, "pin")
            sq = self.tile(st, [128, D], F32, "sq")
            qo = self.rot(st, 2, [128, D], BF16, "qo")
            ko = self.rot(st, 2, [128, 3072], BF16, "ko")
            smr = self.rot(st, 2, [128, NH], F32, "sm")
            r1 = self.tile(st, [128, NH * 16], F32, "r1")
            r2 = self.tile(st, [128, NH * 16], F32, "r2")
            gto = self.rot(st, 2, [128, 3 * NH], F32, "gto")
            posf = self.tile(st, [128, NB], F32, "posf")
            posi = self.tile(st, [128, NB], I32, "posi")
            ang = self.tile(st, [128, NB, 16], F32, "ang")
            kq = self.tile(st, [128, NB, 16], F32, "kq")
            ki = self.tile(st, [128, NB, 16], I32, "ki")
            cos = self.tile(st, [128, NB, 16], F32, "cos")
            sin = self.tile(st, [128, NB, 16], F32, "sin")
            msk = self.tile(st, [128, NB, 16], F32, "msk")
            self.dma(posi[:, :], self.inp["pos"][:, :], [], [posi.r])
            self.cp("dve", posf[:, :], posi[:, :], [posi.r], [posf.r])
            self.tt(ang[:, :, :], posf[:, :].unsqueeze(2).broadcast_to([128, NB, 16]),
                    self.cv("invf").unsqueeze(1).broadcast_to([128, NB, 16]), ALU.mult, [posf.r, self.cst.r], [ang.r])
            TWO_PI = 2.0 * np.pi
            C1 = 6.28125
            C2 = TWO_PI - C1

            def reduce_and_sin(dst, shift):
                self.ts(kq[:, :, :], ang[:, :, :], float(shift), 1.0 / TWO_PI, ALU.add, ALU.mult, [ang.r], [kq.r])
                self.cp("dve", ki[:, :, :], kq[:, :, :], [kq.r], [ki.r])
                self.cp("dve", kq[:, :, :], ki[:, :, :], [ki.r], [kq.r])
                self.stt(dst[:, :, :], kq[:, :, :], -C1, ang[:, :, :], ALU.mult, ALU.add, [kq.r, ang.r], [dst.r])
                self.stt(dst[:, :, :], kq[:, :, :], -C2, dst[:, :, :], ALU.mult, ALU.add, [kq.r, dst.r], [dst.r])
                if shift != 0.0:
                    self.ts(dst[:, :, :], dst[:, :, :], float(shift), None, ALU.add, None, [dst.r], [dst.r])
                self.ts(msk[:, :, :], dst[:, :, :], float(np.pi), -TWO_PI, ALU.is_gt, ALU.mult, [dst.r], [msk.r])
                self.tt(dst[:, :, :], dst[:, :, :], msk[:, :, :], ALU.add, [dst.r, msk.r], [dst.r])
                self.ts(msk[:, :, :], dst[:, :, :], float(-np.pi), TWO_PI, ALU.is_lt, ALU.mult, [dst.r], [msk.r])
                self.tt(dst[:, :, :], dst[:, :, :], msk[:, :, :], ALU.add, [dst.r, msk.r], [dst.r])
                self.ts(dst[:, :, :], dst[:, :, :], float(np.pi), float(-np.pi), ALU.min, ALU.max, [dst.r], [dst.r])
                self.act(dst[:, :, :], dst[:, :, :], AF.Sin, [dst.r], [dst.r])
            reduce_and_sin(sin, 0.0)
            reduce_and_sin(cos, float(np.pi / 2))

            def norm_rope(src, nh, gain, dst, j):
                s3 = src.rearrange("p (h d) -> p h d", d=128)
                q3 = sq[:, 0:nh * 128].rearrange("p (h d) -> p h d", d=128)
                d3 = dst.rearrange("p (h d) -> p h d", d=128)
                sm = smr.next()
                self.tt(q3, s3, s3, ALU.mult, [pin_t.r], [sq.r])
                self.P.op("dve", lambda e, sm=sm: e.tensor_reduce(sm[:, 0:nh], q3, AX.X, ALU.add), [sq.r], [sm.r])
                self.rstd_from_ss(None, sm[:, 0:nh], nh, 128, [sm.r], sm)
                self.tt(q3, s3, sm[:, 0:nh].unsqueeze(2).broadcast_to([128, nh, 128]), ALU.mult, [pin_t.r, sm.r], [sq.r])
                self.tt(q3, q3, gain[:, :].unsqueeze(1).broadcast_to([128, nh, 128]), ALU.mult, [sq.r, gain.r], [sq.r])
                self.cp("act", d3, q3, [sq.r], [dst_t.r])
                cb = cos[:, j, :].unsqueeze(1).broadcast_to([128, nh, 16])
                sb_ = sin[:, j, :].unsqueeze(1).broadcast_to([128, nh, 16])
                a3 = r1[:, 0:nh * 16].rearrange("p (h d) -> p h d", d=16)
                b3 = r2[:, 0:nh * 16].rearrange("p (h d) -> p h d", d=16)
                x1, x2 = q3[:, :, 0:16], q3[:, :, 16:32]
                self.tt(a3, x1, cb, ALU.mult, [sq.r, cos.r], [r1.r])
                self.tt(b3, x2, sb_, ALU.mult, [sq.r, sin.r], [r2.r])
                self.tt(d3[:, :, 0:16], a3, b3, ALU.subtract, [r1.r, r2.r], [dst_t.r])
                self.tt(a3, x2, cb, ALU.mult, [sq.r, cos.r], [r1.r])
                self.tt(b3, x1, sb_, ALU.mult, [sq.r, sin.r], [r2.r])
                self.tt(d3[:, :, 16:32], a3, b3, ALU.add, [r1.r, r2.r], [dst_t.r])

            for j in range(NB):
                pin_t = pin.next()
                self.dma(pin_t[:, :], self.proj[j * 128:(j + 1) * 128, 0:c["NSA_IN"]], [self.proj_r[j]], [pin_t.r])
                dst_t = qo.next()
                norm_rope(pin_t[:, 0:D], NH, qg, dst_t[:, :], j)
                self.dma(qn_d[j * 128:(j + 1) * 128, :], dst_t[:, :], [dst_t.r], [qn_r[j]])
                dst_t = ko.next()
                for i in range(6):
                    o = D + i * 512
                    if i % 2 == 0:
                        norm_rope(pin_t[:, o:o + 512], G, kg[i // 2], dst_t[:, i * 512:(i + 1) * 512], j)
                    else:
                        self.cp("act", dst_t[:, i * 512:(i + 1) * 512], pin_t[:, o:o + 512], [pin_t.r], [dst_t.r])
                self.dma(kvb[j * 128:(j + 1) * 128, :], dst_t[:, :], [dst_t.r], [kvb_rs[j]])
                gtt = gto.next()
                self.act(gtt[:, :], pin_t[:, D + 3072:D + 3072 + 3 * NH], AF.Sigmoid, [pin_t.r], [gtt.r])
                self.dma(gt_d[j * 128:(j + 1) * 128, :], gtt[:, :], [gtt.r], [gt_r[j]])
            self.P.flush()
        kvg, kvg_r = self.pair_gather_rows(kvb, kvb_rs, TL, 3072, BF16, "nkvg")
        self.P.pool_wait_all()
        self.P.flush()
        with ExitStack() as st:
            ksT = self.tile(st, [128, G, T], BF16, "ksT")
            vs = self.tile(st, [128, 2 * NB, 512], BF16, "vs")
            kcmpT = self.tile(st, [128, G, 256], BF16, "kcmpT")
            vcmp = self.tile(st, [128, 2, G * 128], BF16, "vcmp")
            with ExitStack() as s2:
                srcT = self.tile(s2, [128, G, T], BF16, "srcT")
                tok = self.tile(s2, [128, 2 * NB, 512], BF16, "tok")
                w1 = self.tile(s2, [128, 32, 128], BF16, "w1")
                w2f = self.tile(s2, [128, 128], F32, "w2f")
                w2 = self.tile(s2, [128, 128], BF16, "w2")
                posTf = self.tile(s2, [128, 2, 32], F32, "posTf")
                posT = self.tile(s2, [128, 2, 32], BF16, "posT")
                c1 = self.tile(s2, [128, 2], F32, "c1")
                hT = self.tile(s2, [128, 256], BF16, "hT")
                self.load_T(s2, posTf[:, :, :].rearrange("p a i -> p (a i)"), posTf.r, self.inp["nsa_cmp_pos"][0].rearrange("a i d -> (a i) d"), 64)
                self.cp("dve", posT[:, :, :], posTf[:, :, :], [posTf.r], [posT.r])
                w1f = self.tile(s2, [128, 32, 128], F32, "w1f")
                for kvi in range(2):
                    self.load_gathered(tok, kvg, kvg_r, kvi * 512, 512)
                    for gb in range(2 * NB):
                        pb = self.npb()
                        for g in range(G):
                            self.tr(pb[:, g * 128:(g + 1) * 128], tok[:, gb, g * 128:(g + 1) * 128], self.identb[:, :], [tok.r], [pb.r])
                        self.cp("act" if gb % 2 else "dve", srcT[:, :, gb * 128:(gb + 1) * 128],
                                pb[:, 0:512].rearrange("p (a b) -> p a b", b=128), [pb.r], [srcT.r])
                    self.dma(w1f[:, :, :], self.inp["nsa_cmp_w1"][0, kvi].rearrange("(i d) o -> d i o", d=128), [], [w1f.r])
                    self.cp("dve", w1[:, :, :], w1f[:, :, :], [w1f.r], [w1.r])
                    self.dma(w2f[:, :], self.inp["nsa_cmp_w2"][0, kvi], [], [w2f.r])
                    self.cp("dve", w2[:, :], w2f[:, :], [w2f.r], [w2.r])
                    ps = self.npf()
                    for i in range(32):
                        self.mm(ps[:, 0:1], w1[:, i, :], posT[:, kvi, i:i + 1], i == 0, i == 31, [w1.r, posT.r], [ps.r])
                    self.cp("dve", c1[:, kvi:kvi + 1], ps[:, 0:1], [ps.r], [c1.r])
                    for g in range(G):
                        ps = self.npf()
                        for i in range(32):
                            self.mm(ps[:, 0:NCMP], w1[:, i, :], srcT[:, g, i:i + 16 * (NCMP - 1) + 1:16], i == 0, i == 31, [w1.r, srcT.r], [ps.r])
                        self.act(hT[:, 0:NCMP], ps[:, 0:NCMP], AF.Silu, [ps.r, c1.r], [hT.r], bias=c1[:, kvi:kvi + 1], scale=1.0)
                        if kvi == 0:
                            p2 = self.npf()
                            self.mm(p2[:, 0:NCMP], w2[:, :], hT[:, 0:NCMP], True, True, [w2.r, hT.r], [p2.r])
                            self.cp("dve", kcmpT[:, g, 0:NCMP], p2[:, 0:NCMP], [p2.r], [kcmpT.r])
                        else:
                            for cch in range((NCMP + 127) // 128):
                                rows = min(128, NCMP - cch * 128)
                                p2 = self.npf()
                                self.mm(p2[0:rows, 0:128], hT[:, cch * 128:cch * 128 + rows], w2[:, :], True, True, [w2.r, hT.r], [p2.r])
                                self.cp("dve", vcmp[0:rows, cch, g * 128:(g + 1) * 128], p2[0:rows, 0:128], [p2.r], [vcmp.r])
                self.load_gathered(tok, kvg, kvg_r, 2 * 512, 512)
                for gb in range(2 * NB):
                    pb = self.npb()
                    for g in range(G):
                        self.tr(pb[:, g * 128:(g + 1) * 128], tok[:, gb, g * 128:(g + 1) * 128], self.identb[:, :], [tok.r], [pb.r])
                    self.cp("act" if gb % 2 else "dve", ksT[:, :, gb * 128:(gb + 1) * 128],
                            pb[:, 0:512].rearrange("p (a b) -> p a b", b=128), [pb.r], [ksT.r])
                self.load_gathered(vs, kvg, kvg_r, 3 * 512, 512)
                self.P.flush()
            NCC = (NCMP + 127) // 128
            qin = self.rot(st, 1, [128, D], BF16, "qin")
            qT = self.tile(st, [128, NH, 128], BF16, "qT")
            gts = self.rot(st, 2, [128, 3 * NH], F32, "gts")
            kwt = self.rot(st, 1, [128, 6, 512], BF16, "kwt")
            vwt = self.rot(st, 1, [128, 6, 512], BF16, "vwt")
            cmi = self.tile(st, [128, 256], F32, "cmi")
            kwT = self.tile(st, [128, G, 768], BF16, "kwT")
            mc = self.tile(st, [128, 256], F32, "mc")
            wm = self.tile(st, [128, 768], F32, "wm")
            wm2 = self.tile(st, [128, 768], F32, "wm2")
            nf = self.tile(st, [128, 64], F32, "nf")
            addt = self.tile(st, [128, 64], F32, "addt")
            t64 = self.tile(st, [128, 64], F32, "t64")
            imp = self.tile(st, [128, 264], F32, "imp")
            isel = self.tile(st, [128, 64], F32, "isel")
            sc = self.tile(st, [128, 64], F32, "sc")
            sc2 = self.tile(st, [128, 64], F32, "sc2")
            m8 = self.tile(st, [128, 16], F32, "m8")
            sel = self.tile(st, [128, 64], F32, "sel")
            Mg = self.tile(st, [128, T], BF16, "Mg")
            Pf = self.tile(st, [128, T], F32, "Pf")
            Pb = self.tile(st, [128, T], BF16, "Pb")
            PT = self.tile(st, [128, 2 * NB, 128], BF16, "PT")
            pc = self.tile(st, [128, 256], F32, "pc")
            pcb = self.tile(st, [128, 256], BF16, "pcb")
            smr = self.rot(st, 6, [128, 8], F32, "sm")
            oacc = self.tile(st, [128, NH, 128], F32, "oacc")
            orot = self.rot(st, 1, [128, D], BF16, "o")
            self.memset(imp[:, :], 0.0, [imp.r])
            for j in range(NB):
                nkb = 2 * j + 2
                nk = nkb * 128
                wb0 = max(0, 2 * j - 4)
                nwb = nkb - wb0
                nw = nwb * 128
                qi, gt = qin.next(), gts.next()
                self.dma(qi[:, :], qn_d[j * 128:(j + 1) * 128, :], [qn_r[j]], [qi.r])
                self.dma(gt[:, :], gt_d[j * 128:(j + 1) * 128, :], [gt_r[j]], [gt.r])
                self.transpose_into(qi, D, qT, 0)
                kw_, vw_ = kwt.next(), vwt.next()
                for bi in range(nwb):
                    r0 = self.grow(wb0 + bi)
                    self.dma(kw_[:, bi, :], kvg.ap()[r0:r0 + 128, 4 * 512:5 * 512], [kvg_r], [kw_.r])
                    self.dma(vw_[:, bi, :], kvg.ap()[r0:r0 + 128, 5 * 512:6 * 512], [kvg_r], [vw_.r])
                for bi in range(nwb):
                    pb = self.npb()
                    for g in range(G):
                        self.tr(pb[:, g * 128:(g + 1) * 128], kw_[:, bi, g * 128:(g + 1) * 128], self.identb[:, :], [kw_.r], [pb.r])
                    self.cp("act" if bi % 2 else "dve", kwT[:, :, bi * 128:(bi + 1) * 128],
                            pb[:, 0:512].rearrange("p (a b) -> p a b", b=128), [pb.r], [kwT.r])
                qp = self.qpos[:, j:j + 1]
                smq = smr.next()
                self.ts(smq[:, 4:5], qp, float(-wb0 * 128), None, ALU.add, None, [self.qpos.r], [smq.r])
                self.ts(smq[:, 5:6], self.qpm[:, j:j + 1], float(-wb0 * 128), None, ALU.add, None, [self.qpm.r], [smq.r])
                self.ts(smq[:, 6:7], qp, float(-2 * j * 128), None, ALU.add, None, [self.qpos.r], [smq.r])
                self.ts(mc[:, :], self.cv("cmpend"), qp, None, ALU.is_le, None, [self.cst.r, self.qpos.r], [mc.r])
                self.ts(wm[:, 0:nw], self.kio[:, 0:nw], smq[:, 4:5], None, ALU.is_le, None, [self.cst.r, smq.r], [wm.r])
                self.ts(wm2[:, 0:nw], self.kio[:, 0:nw], smq[:, 5:6], None, ALU.is_ge, None, [self.cst.r, smq.r], [wm2.r])
                self.tt(wm[:, 0:nw], wm[:, 0:nw], wm2[:, 0:nw], ALU.mult, [wm.r, wm2.r], [wm.r])
                self.ts(cmi[:, :], self.kio[:, 0:256], smq[:, 6:7], None, ALU.is_le, None, [self.cst.r, smq.r], [cmi.r])
                cur = self.cur[:, j:j + 1]
                mi = self.cv("miota")
                self.ts(t64[:, :], mi, cur, None, ALU.is_equal, None, [self.cst.r, self.cur.r], [t64.r])
                self.tt(t64[:, :], t64[:, :], self.cv("eq0"), ALU.max, [t64.r, self.cst.r], [t64.r])
                self.ts(nf[:, :], mi, 1.0, cur, ALU.add, ALU.is_equal, [self.cst.r, self.cur.r], [nf.r])
                self.tt(t64[:, :], t64[:, :], nf[:, :], ALU.max, [t64.r, nf.r], [t64.r])
                self.ts(nf[:, :], mi, cur, None, ALU.is_le, None, [self.cst.r, self.cur.r], [nf.r])
                self.tt(addt[:, :], t64[:, :], nf[:, :], ALU.add, [t64.r, nf.r], [addt.r])
                self.ts(addt[:, :], addt[:, :], -1.0, BIG, ALU.add, ALU.mult, [addt.r], [addt.r])
                self.tt(nf[:, :], nf[:, :], t64[:, :], ALU.subtract, [nf.r, t64.r], [nf.r])
                ot = orot.next()
                for g in range(G):
                    for jh in range(HPG):
                        h = g * HPG + jh
                        ps = self.npf()
                        self.mm(ps[:, 0:NCMP], qT[:, h, :], kcmpT[:, g, 0:NCMP], True, True, [qT.r, kcmpT.r], [ps.r])
                        self.act(pc[:, 0:NCMP], ps[:, 0:NCMP], AF.Exp, [ps.r], [pc.r], scale=scale)
                        sm = smr.next()
                        self.stt(pc[:, 0:NCMP], pc[:, 0:NCMP], 1.0, mc[:, 0:NCMP], ALU.mult, ALU.mult, [pc.r, mc.r], [pc.r, sm.r], accum=sm[:, 0:1])
                        self.ts(sm[:, 1:2], sm[:, 0:1], 1e-30, None, ALU.add, None, [sm.r], [sm.r])
                        self.P.op("dve", lambda e, sm=sm: e.reciprocal(sm[:, 1:2], sm[:, 1:2]), [sm.r], [sm.r])
                        self.ts(pc[:, 0:NCMP], pc[:, 0:NCMP], sm[:, 1:2], None, ALU.mult, None, [pc.r, sm.r], [pc.r])
                        if jh == 0:
                            self.cp("dve", imp[:, 1:1 + NCMP], pc[:, 0:NCMP], [pc.r], [imp.r])
                        else:
                            self.tt(imp[:, 1:1 + NCMP], imp[:, 1:1 + NCMP], pc[:, 0:NCMP], ALU.add, [imp.r, pc.r], [imp.r])
                        self.cp("act", pcb[:, 0:NCMP], pc[:, 0:NCMP], [pc.r], [pcb.r])
                        pbk = self.npb()
                        for cch in range(NCC):
                            rows = min(128, NCMP - cch * 128)
                            self.tr(pbk[0:rows, cch * 128:(cch + 1) * 128], pcb[:, cch * 128:cch * 128 + rows], self.identb[:, :], [pcb.r], [pbk.r])
                        for cch in range(NCC):
                            rows = min(128, NCMP - cch * 128)
                            self.cp("dve", PT[0:rows, cch, :], pbk[0:rows, cch * 128:(cch + 1) * 128], [pbk.r], [PT.r])
                        po = self.npf()
                        for cch in range(NCC):
                            rows = min(128, NCMP - cch * 128)
                            self.mm(po[:, 0:128], PT[0:rows, cch, :], vcmp[0:rows, cch, g * 128:(g + 1) * 128], cch == 0, cch == NCC - 1,
                                    [PT.r, vcmp.r], [po.r])
                        self.ts(oacc[:, h, :], po[:, 0:128], gt[:, 3 * h:3 * h + 1], None, ALU.mult, None, [po.r, gt.r], [oacc.r])
                    self.P.op("dve", lambda e: e.tensor_reduce(isel[:, 0:NSEL], imp[:, 0:4 * NSEL].rearrange("p (m k) -> p m k", k=4), AX.X, ALU.add), [imp.r], [isel.r])
                    self.tt(isel[:, 0:NSEL], isel[:, 0:NSEL], imp[:, 4:4 * NSEL + 1:4], ALU.add, [isel.r, imp.r], [isel.r])
                    self.tt(sc[:, 0:NSEL], isel[:, 0:NSEL], nf[:, 0:NSEL], ALU.mult, [isel.r, nf.r], [sc.r])
                    self.tt(sc[:, 0:NSEL], sc[:, 0:NSEL], addt[:, 0:NSEL], ALU.add, [sc.r, addt.r], [sc.r])
                    self.P.op("dve", lambda e: e.max(m8[:, 0:8], sc[:, 0:NSEL]), [sc.r], [m8.r])
                    self.P.op("dve", lambda e: e.match_replace(sc2[:, 0:NSEL], m8[:, 0:8], sc[:, 0:NSEL], -3.0e38), [sc.r, m8.r], [sc2.r])
                    self.P.op("dve", lambda e: e.max(m8[:, 8:16], sc2[:, 0:NSEL]), [sc2.r], [m8.r])
                    self.ts(sel[:, 0:NSEL], sc[:, 0:NSEL], m8[:, 15:16], None, ALU.is_ge, None, [sc.r, m8.r], [sel.r])
                    self.cp("dve", Mg[:, 0:nk].rearrange("p (m k) -> p m k", k=64),
                            sel[:, 0:nkb * 2].unsqueeze(2).broadcast_to([128, nkb * 2, 64]), [sel.r], [Mg.r])
                    self.tt(Mg[:, nk - 256:nk], Mg[:, nk - 256:nk], cmi[:, :], ALU.mult, [Mg.r, cmi.r], [Mg.r])
                    for jh in range(HPG):
                        h = g * HPG + jh
                        for k0 in range(0, nk, 512):
                            w = min(512, nk - k0)
                            ps = self.npf()
                            self.mm(ps[:, 0:w], qT[:, h, :], ksT[:, g, k0:k0 + w], True, True, [qT.r, ksT.r], [ps.r])
                            self.act(Pf[:, k0:k0 + w], ps[:, 0:w], AF.Exp, [ps.r], [Pf.r], scale=scale)
                        sm = smr.next()
                        self.stt(Pb[:, 0:nk], Pf[:, 0:nk], 1.0, Mg[:, 0:nk], ALU.mult, ALU.mult, [Pf.r, Mg.r], [Pb.r, sm.r], accum=sm[:, 0:1])
                        self.P.op("dve", lambda e, sm=sm: e.reciprocal(sm[:, 1:2], sm[:, 0:1]), [sm.r], [sm.r])
                        self.tt(sm[:, 2:3], sm[:, 1:2], gt[:, 3 * h + 1:3 * h + 2], ALU.mult, [sm.r, gt.r], [sm.r])
                        self.transpose_into(Pb, nk, PT, 0)
                        po = self.npf()
                        for kb in range(nkb):
                            self.mm(po[:, 0:128], PT[:, kb, :], vs[:, kb, g * 128:(g + 1) * 128], kb == 0, kb == nkb - 1, [PT.r, vs.r], [po.r])
                        self.stt(oacc[:, h, :], po[:, 0:128], sm[:, 2:3], oacc[:, h, :], ALU.mult, ALU.add, [po.r, sm.r, oacc.r], [oacc.r])
                        for k0 in range(0, nw, 512):
                            w = min(512, nw - k0)
                            ps = self.npf()
                            self.mm(ps[:, 0:w], qT[:, h, :], kwT[:, g, k0:k0 + w], True, True, [qT.r, kwT.r], [ps.r])
                            self.act(Pf[:, k0:k0 + w], ps[:, 0:w], AF.Exp, [ps.r], [Pf.r], scale=scale)
                        sm = smr.next()
                        self.stt(Pb[:, 0:nw], Pf[:, 0:nw], 1.0, wm[:, 0:nw], ALU.mult, ALU.mult, [Pf.r, wm.r], [Pb.r, sm.r], accum=sm[:, 0:1])
                        self.P.op("dve", lambda e, sm=sm: e.reciprocal(sm[:, 1:2], sm[:, 0:1]), [sm.r], [sm.r])
                        self.tt(sm[:, 2:3], sm[:, 1:2], gt[:, 3 * h + 2:3 * h + 3], ALU.mult, [sm.r, gt.r], [sm.r])
                        self.transpose_into(Pb, nw, PT, 0)
                        po = self.npf()
                        for kb in range(nwb):
                            self.mm(po[:, 0:128], PT[:, kb, :], vw_[:, kb, g * 128:(g + 1) * 128], kb == 0, kb == nwb - 1, [PT.r, vw_.r], [po.r])
                        self.stt(oacc[:, h, :], po[:, 0:128], sm[:, 2:3], oacc[:, h, :], ALU.mult, ALU.add, [po.r, sm.r, oacc.r], [oacc.r])
                self.cp("act", ot[:, :].rearrange("p (h d) -> p h d", d=128), oacc[:, :, :], [oacc.r], [ot.r])
                self.dma(self.cat[j * 128:(j + 1) * 128, 0:D], ot[:, :], [ot.r], [self.cat_r[j]])
            self.P.flush()

    def cv(self, name):
        o, n = self.cl[name]
        return self.cst[:, o:o + n]

    def build(self, depth=4):
        c = self.c
        nc = self.nc
        D, T, DFF, TL, NB, KC, MEMW, OUTIN = c["D"], c["T"], c["DFF"], c["TL"], c["NB"], c["KC"], c["MEMW"], c["OUTIN"]
        self.CWD = 256
        self.cl = cst_layout(T)
        inp = {}
        kinds = set(self.kinds[i] for i in range(depth))
        inp["x"] = self.din("x", [TL, D])
        inp["mem"] = self.din("mem", [256, D])
        if 0 in kinds:
            inp["pos"] = self.din("pos", [128, NB], I32)
        inp["qposd"] = self.din("qposd", [128, 3 * NB])
        if depth > 0:
            inp["hfd"] = self.din("hfd", [128, 1])
        inp["cstd"] = self.din("cstd", [128, self.cl["_w"]])
        small = dict(ffn1_norm=[depth, D], ffn2_norm=[depth, D], mix_norm=[depth, D], mem_norm=[D], mem_k_norm=[c["MHD"]],
                     mem_q_norm=[depth, c["MHD"]], nsa_q_norm=[1, 128], nsa_k_norm=[1, 3, 128], nsa_cmp_pos=[1, 2, 32, 128],
                     nsa_cmp_w2=[1, 2, 128, 128], nsa_cmp_w1=[1, 2, 4096, 128], conv_b_in=[1, 2 * D], conv_dw_w=[1, 31, D], conv_dw_b=[1, D],
                     conv_ln_g=[1, D], conv_ln_b=[1, D], gm_ln_g=[1, D], gm_ln_b=[1, D], gm_ws=[1, c["NH"], 128, 128],
                     gm_bs=[1, c["NH"], 128])
        pref = {"nsa": 0, "conv": 1, "sb": 2, "gm": 3}

        def wanted(k):
            p = k.split("_")[0]
            if p in pref:
                return pref[p] in kinds
            return depth > 0 or k.startswith("mem")
        for k, shp in small.items():
            if wanted(k):
                inp[k] = self.din(k, shp)
        wdefs = dict(ffn1_w_gu=(depth, D, 2 * DFF), ffn1_w_down=(depth, DFF, D), ffn2_w_gu=(depth, D, 2 * DFF),
                     ffn2_w_down=(depth, DFF, D), mem_w_kv=(1, D, 2 * MEMW), nsa_w_in=(1, D, c["NIN"][0]),
                     conv_w_in=(1, D, c["NIN"][1]), sb_w_in=(1, D, c["NIN"][2]), gm_w_in=(1, D, c["NIN"][3]),
                     nsa_w_out=(1, OUTIN, D), conv_w_out=(1, OUTIN, D), sb_w_out=(1, OUTIN, D), gm_w_out=(1, OUTIN, D))
        self.wdefs = wdefs
        wd = {}
        for k, (L, K, N) in wdefs.items():
            if wanted(k):
                wd[k] = self.din(k, [L * (K if NOCC else K // NCORES), N])
        self.declared = set(inp.keys()) | set(wd.keys())
        if depth == 0:
            self.declared -= {"pos", "hfd"}
        self.inp = inp
        self.out = nc.dram_tensor("out", [TL, D], F32, kind="ExternalOutput").ap()
        self.wsrc = {}
        mixn = ["nsa", "conv", "sb", "gm"]
        for i in range(depth):
            for w in (1, 2):
                self.wsrc[("gu", w, i)] = (wd["ffn%d_w_gu" % w], i * (D // NCORES), D // NCORES, 2 * DFF)
                self.wsrc[("down", w, i)] = (wd["ffn%d_w_down" % w], i * (DFF // NCORES), DFF // NCORES, D)
            self.wsrc[("in", i)] = (wd[mixn[self.kinds[i]] + "_w_in"], 0, D // NCORES, c["NIN"][self.kinds[i]])
            self.wsrc[("out", i)] = (wd[mixn[self.kinds[i]] + "_w_out"], 0, OUTIN // NCORES, D)
        self.wsrc[("memkv",)] = (wd["mem_w_kv"], 0, D // NCORES, 2 * MEMW)
        self.wfull = {}
        self.xres_r = {}
        self.xres = self.dscr("xres", [TL, D], F32)
        self.cat = self.dscr("cat", [TL, OUTIN], BF16)
        self.cat_r = [Res("cat") for _ in range(NB)]

        es = self.es
        with es:
            self.P = Prog(nc, es)
            self.pf = Rot([Tile(es.enter_context(nc.psum_tensor(self.name("pf"), [128, 512], F32)), "pf") for _ in range(6)])
            self.pb = Rot([Tile(es.enter_context(nc.psum_tensor(self.name("pb"), [128, 1024], BF16)), "pb") for _ in range(2)])
            g = es
            self.cst = self.tile(g, [128, self.cl["_w"]], F32, "cst")
            self.dma(self.cst[:, :], inp["cstd"][:, :], [], [self.cst.r])
            self.identf = self.cv("ident")
            self.tril = self.cv("tril")
            self.kio = self.cv("kiota")
            identb = self.tile(g, [128, 128], BF16, "identb")
            self.cp("dve", identb[:, :], self.cv("ident"), [self.cst.r], [identb.r])
            self.identb = identb
            qp3 = self.tile(g, [128, 3 * NB], F32, "qp3")
            self.dma(qp3[:, :], inp["qposd"][:, :], [], [qp3.r])
            for nm, o in (("qpos", 0), ("qpm", NB), ("cur", 2 * NB)):
                v = Tile.__new__(Tile)
                v.r = qp3.r
                v.t = qp3.t[:, o:o + NB]
                setattr(self, nm, v)
            self.hf = self.tile(g, [128, 1], F32, "hf")
            if depth > 0:
                self.dma(self.hf[:, :], inp["hfd"][:, :], [], [self.hf.r])
            self.epsb = self.tile(g, [128, 1], F32, "epsb")
            self.memset(self.epsb[:, :], EPS, [self.epsb.r])
            self.oneb = self.tile(g, [128, 1], F32, "oneb")
            self.memset(self.oneb[:, :], 1.0, [self.oneb.r])
            self.mkT = self.tile(g, [128, c["MH"] * c["MDC"], 256], BF16, "mkT")
            self.mv = self.tile(g, [128, 2, MEMW], BF16, "mv")
            for gs in range(NB):
                self.P.dma("sp", (lambda gs: (lambda e: e.dma_start(out=self.xres[gs * 128:(gs + 1) * 128, :], in_=inp["x"][gs * 128:(gs + 1) * 128, :])))(gs),
                           [], self.x_all_res(gs))
            order = [("memkv",)]
            for i in range(depth):
                order += [("gu", 1, i), ("down", 1, i), ("in", i)]
                order += [("out", i), ("gu", 2, i), ("down", 2, i)]
            self.worder = order
            self.wpos = 0

            def prefetch(upto_key):
                while self.wpos < len(order):
                    k = order[self.wpos]
                    self.gather_weight(k)
                    self.wpos += 1
                    if k == upto_key:
                        break
            import os
            stop = os.environ.get("KSTOP", "")
            if stop:
                tch = self.tile(g, [1, 4], F32, "tch")
                tci = self.tile(g, [1, 4], I32, "tci")
                for k, apx in list(inp.items()) + list(wd.items()):
                    flat = apx
                    while len(flat.shape) > 1:
                        flat = flat[0]
                    dst = tci if k == "pos" else tch
                    self.dma(dst[0:1, 0:1], flat[0:1].unsqueeze(0), [], [dst.r])
            self.P.flush()
            prefetch(("zzz",))
            self.P.pool_wait_all()
            self.P.flush()
            steps = [("memkv", self.phase_memkv, None)]
            for i in range(depth):
                kind = self.kinds[i]
                steps.append(("ffn1_%d" % i, (lambda i=i: self.phase_ffn(i, 1, False)), ("in", i)))
                steps.append(("inproj_%d" % i, (lambda i=i: self.phase_inproj(i)), ("out", i)))
                mix = [self.phase_nsa, self.phase_conv, self.phase_sb, self.phase_gmlp][kind]
                steps.append(("mix_%d" % i, mix, ("down", 2, i)))
                steps.append(("memattn_%d" % i, (lambda i=i: self.phase_memattn(i)), ("down", 1, i + 1) if i + 1 < depth else None))
                steps.append(("outproj_%d" % i, (lambda i=i: self.phase_outproj(i)), None))
                steps.append(("ffn2_%d" % i, (lambda i=i: self.phase_ffn(i, 2, i == depth - 1 and not stop)), None))
            stopped = False
            for nm, fn, pf in steps:
                fn()
                if nm == stop:
                    stopped = True
                    break
            if stopped or depth == 0:
                for gs in range(NB):
                    self.P.dma("sp", (lambda gs: (lambda e: e.dma_start(out=self.out[gs * 128:(gs + 1) * 128, :], in_=self.xres[gs * 128:(gs + 1) * 128, :])))(gs),
                               self.x_all_res(gs), [Res()], True)
            self.P.flush(final=True)
        return nc


def host_consts(cfg):
    T = cfg["T"]
    cl = cst_layout(T)
    cst = np.zeros((128, cl["_w"]), np.float32)
    o, n = cl["ident"]
    cst[:, o:o + n] = np.eye(128, dtype=np.float32)
    o, n = cl["tril"]
    cst[:, o:o + n] = np.tril(np.ones((128, 128), np.float32))
    o, n = cl["kiota"]
    cst[:, o:o + n] = np.arange(768, dtype=np.float32)[None, :]
    o, n = cl["cmpend"]
    ce = np.arange(256, dtype=np.float32) * 16 + 31
    ce[cfg["NCMP"]:] = 1e9
    cst[:, o:o + n] = ce[None, :]
    o, n = cl["miota"]
    cst[:, o:o + n] = np.arange(64, dtype=np.float32)[None, :]
    o, n = cl["eq0"]
    cst[:, o] = 1.0
    o, n = cl["invf"]
    cst[:, o:o + n] = (1.0 / (np.float32(500000.0) ** (np.arange(0, 32, 2, dtype=np.float32) / np.float32(32))))[None, :].astype(np.float32)
    return cst


def make_in_maps(cfg, inputs, depth=4, declared=None):
    NB, TL, D = cfg["NB"], cfg["TL"], cfg["D"]
    cst = host_consts(cfg)
    x = np.asarray(inputs["x"])
    B = x.shape[0]
    xb = x.reshape(B, NB, 2, 128, D)
    pos = np.asarray(inputs["positions"]).reshape(B, NB, 2, 128)
    small = ["ffn1_norm", "ffn2_norm", "mix_norm", "mem_norm", "mem_k_norm", "mem_q_norm", "nsa_q_norm", "nsa_k_norm",
             "nsa_cmp_pos", "nsa_cmp_w2", "nsa_cmp_w1", "conv_b_in", "conv_dw_w", "conv_dw_b", "conv_ln_g", "conv_ln_b", "gm_ln_g",
             "gm_ln_b", "gm_ws", "gm_bs"]
    wnames = ["ffn1_w_gu", "ffn1_w_down", "ffn2_w_gu", "ffn2_w_down", "mem_w_kv", "nsa_w_in", "conv_w_in", "sb_w_in",
              "gm_w_in", "nsa_w_out", "conv_w_out", "sb_w_out", "gm_w_out"]
    in_maps = []
    for cidx in range(NCORES):
        b, hf = cidx // 2, cidx % 2
        m = {}
        m["x"] = np.ascontiguousarray(xb[b, :, hf]).reshape(TL, D)
        m["mem"] = np.ascontiguousarray(np.asarray(inputs["mem"])[b])
        m["pos"] = np.ascontiguousarray(pos[b, :, hf].T).astype(np.int32)
        qpos = ((2 * np.arange(NB)[None, :] + hf) * 128 + np.arange(128)[:, None]).astype(np.float32)
        m["qposd"] = np.concatenate([qpos, qpos - 511.0, np.floor(qpos / 64.0)], axis=1).astype(np.float32)
        m["hfd"] = np.full((128, 1), float(hf), np.float32)
        m["cstd"] = cst
        for k in small:
            m[k] = np.ascontiguousarray(np.asarray(inputs[k]), dtype=np.float32)
        for k in wnames:
            w = np.asarray(inputs[k])
            if k == "mem_w_kv":
                w = w[None]
            L, K, N = w.shape
            r = K // NCORES
            g = pick_g(K, N)
            if NOCC:
                m[k] = np.ascontiguousarray(w).reshape(L * K, N)
            else:
                m[k] = np.ascontiguousarray(w.reshape(L, K // g, g, N)[:, cidx::NCORES]).reshape(L * r, N)
        if declared is not None:
            m = {k: v for k, v in m.items() if k in declared}
        in_maps.append(m)
    return in_maps


def assemble(cfg, results, B):
    NB, D = cfg["NB"], cfg["D"]
    out = np.empty((B, NB, 2, 128, D), np.float32)
    for cidx in range(NCORES):
        b, hf = cidx // 2, cidx % 2
        out[b, :, hf] = np.asarray(results[cidx]["out"]).reshape(NB, 128, D)
    return out.reshape(B, NB * 256, D)


_NC_CACHE = {}


def kernel(**inputs):
    cfg = make_cfg()
    if "nc" not in _NC_CACHE:
        _NC_CACHE["nc"] = Builder(cfg).build(4)
    nc = _NC_CACHE["nc"]
    in_maps = make_in_maps(cfg, inputs)
    res = run_bass_kernel_spmd(nc, in_maps, core_ids=list(range(NCORES)))
    return assemble(cfg, res.results, 4)
```
